# Optimizing a Trainium2 kernel written in Bass

```python
import math
import jax, jax.numpy as jnp
from jax import lax
import numpy as np

D_MODEL = 1024
BATCH = 16
SEQ = 2048
DEPTH = 2

SSD_WIDTH = D_MODEL
SSD_HEAD_DIM = 64
SSD_HEADS = SSD_WIDTH // SSD_HEAD_DIM
SSD_GROUPS = 2
SSD_STATE = 128
SSD_CONV = 4
SSD_CHUNK = 128
SSD_CONV_DIM = SSD_WIDTH + 2 * SSD_GROUPS * SSD_STATE
ATTN_HEAD_DIM = 64
ATTN_WIDTH = D_MODEL // 2
ATTN_Q_HEADS = ATTN_WIDTH // ATTN_HEAD_DIM
ATTN_KV_HEADS = 2
WINDOW = 128
CONF_WIDTH = D_MODEL // 2
CONF_KERNEL = 31
MIX_WIDTH = SSD_WIDTH + ATTN_WIDTH + CONF_WIDTH
IN_SPLITS = (MIX_WIDTH, SSD_CONV_DIM, SSD_HEADS, ATTN_Q_HEADS * ATTN_HEAD_DIM,
             ATTN_KV_HEADS * ATTN_HEAD_DIM, ATTN_KV_HEADS * ATTN_HEAD_DIM, 2 * CONF_WIDTH)
D_IN_PROJ = sum(IN_SPLITS)
EPS = 1e-5

kernel_name = "hybrid_ssd_swa_conformer_parallel_heads"


def _split(a, sizes):
    idx = np.cumsum(sizes)[:-1].tolist()
    return jnp.split(a, idx, axis=-1)


def rmsnorm(x, w):
    xf = x.astype(jnp.float32)
    y = xf * lax.rsqrt(jnp.mean(xf * xf, axis=-1, keepdims=True) + EPS)
    return (y * w.astype(jnp.float32)).astype(x.dtype)


def gated_group_rmsnorm(y, z, w, groups):
    g = (y * jax.nn.silu(z)).astype(jnp.float32)
    shp = g.shape
    g = g.reshape(shp[:-1] + (groups, shp[-1] // groups))
    g = g * lax.rsqrt(jnp.mean(g * g, axis=-1, keepdims=True) + EPS)
    return (g.reshape(shp) * w.astype(jnp.float32)).astype(y.dtype)


def layernorm(x, w, b):
    xf = x.astype(jnp.float32)
    mu = jnp.mean(xf, axis=-1, keepdims=True)
    xc = xf - mu
    y = xc * lax.rsqrt(jnp.mean(xc * xc, axis=-1, keepdims=True) + EPS)
    return (y * w.astype(jnp.float32) + b.astype(jnp.float32)).astype(x.dtype)


def causal_depthwise_conv(x, w, b):
    K, C = w.shape
    y = lax.conv_general_dilated(x, w[:, None, :].astype(x.dtype), window_strides=(1,),
                                 padding=[(K - 1, 0)], dimension_numbers=('NWC', 'WIO', 'NWC'),
                                 feature_group_count=C)
    return y + b.astype(x.dtype)


def ssd_chunked(x, dt, A, B, C):
    b, l, h, p = x.shape
    g, n = B.shape[2], B.shape[3]
    r = h // g
    nc = l // SSD_CHUNK
    Q = SSD_CHUNK
    x = x.reshape(b, nc, Q, g, r, p)
    dt = dt.reshape(b, nc, Q, g, r)
    B = B.reshape(b, nc, Q, g, n)
    C = C.reshape(b, nc, Q, g, n)
    a = dt * A.reshape(g, r)
    a_cs = jnp.cumsum(a, axis=2)
    xdt = x * dt[..., None]
    seg = a_cs[:, :, :, None] - a_cs[:, :, None, :]
    causal = jnp.tril(jnp.ones((Q, Q), dtype=bool))[None, None, :, :, None, None]
    L = jnp.exp(jnp.where(causal, seg, -jnp.inf))
    cb = jnp.einsum('bcign,bcjgn->bcijg', C, B)
    M = cb[..., None] * L
    y_diag = jnp.einsum('bcijgr,bcjgrp->bcigrp', M, xdt)
    decay_to_end = jnp.exp(a_cs[:, :, -1:] - a_cs)
    states = jnp.einsum('bcjgn,bcjgrp->bcgrpn', B, xdt * decay_to_end[..., None])
    chunk_decay = jnp.exp(a_cs[:, :, -1])

    def step(S, inp):
        st, dec = inp
        return dec[..., None, None] * S + st, S

    S0 = jnp.zeros((b, g, r, p, n), jnp.float32)
    _, prev = lax.scan(step, S0, (jnp.moveaxis(states, 1, 0), jnp.moveaxis(chunk_decay, 1, 0)))
    prev = jnp.moveaxis(prev, 0, 1)
    y_off = jnp.einsum('bcign,bcgrpn->bcigrp', C, prev) * jnp.exp(a_cs)[..., None]
    return (y_diag + y_off).reshape(b, l, h, p)


def swa_gqa_sinks(q, k, v, sinks):
    b, l, g, r, d = q.shape
    W = WINDOW
    nb = l // W
    qb = q.reshape(b, nb, W, g, r, d)
    kb = k.reshape(b, nb, W, g, d)
    vb = v.reshape(b, nb, W, g, d)
    pad = ((0, 0), (1, 0), (0, 0), (0, 0), (0, 0))
    kk = jnp.concatenate([jnp.pad(kb, pad)[:, :-1], kb], axis=2)
    vv = jnp.concatenate([jnp.pad(vb, pad)[:, :-1], vb], axis=2)
    s = jnp.einsum('bnqgrd,bnkgd->bngrqk', qb, kk,
                   preferred_element_type=jnp.float32) * (d ** -0.5)
    qi = jnp.arange(W)[:, None]
    kj = jnp.arange(2 * W)[None, :] - W
    rel = qi - kj
    band = (rel >= 0) & (rel < W)
    mask = band[None] & ((jnp.arange(nb)[:, None, None] > 0) | (kj[None] >= 0))
    s = jnp.where(mask[None, :, None, None], s, -jnp.inf)
    sk = sinks.astype(jnp.float32).reshape(g, r)[None, None, :, :, None, None]
    lse = jnp.logaddexp(jax.nn.logsumexp(s, axis=-1, keepdims=True), sk)
    pr = jnp.exp(s - lse)
    o = jnp.einsum('bngrqk,bnkgd->bnqgrd', pr.astype(v.dtype), vv)
    return o.reshape(b, l, g * r * d)


def hybrid_mixer(h, w_in, conv_w, conv_b, dt_bias, a_log, d_skip, ssd_norm_w, sinks,
                 dw_w, dw_b, ln_w, ln_b, w_out):
    b, l, _ = h.shape
    proj = h @ w_in
    z, xbc, dt, q, k, v, conf = _split(proj, IN_SPLITS)
    z_ssd, z_attn, z_conf = _split(z, (SSD_WIDTH, ATTN_WIDTH, CONF_WIDTH))
    xbc = jax.nn.silu(causal_depthwise_conv(xbc, conv_w, conv_b))
    xs, Bs, Cs = _split(xbc, (SSD_WIDTH, SSD_GROUPS * SSD_STATE, SSD_GROUPS * SSD_STATE))
    xs = xs.reshape(b, l, SSD_HEADS, SSD_HEAD_DIM).astype(jnp.float32)
    Bs = Bs.reshape(b, l, SSD_GROUPS, SSD_STATE).astype(jnp.float32)
    Cs = Cs.reshape(b, l, SSD_GROUPS, SSD_STATE).astype(jnp.float32)
    dtp = jax.nn.softplus(dt.astype(jnp.float32) + dt_bias.astype(jnp.float32))
    A = -jnp.exp(a_log.astype(jnp.float32))
    y = ssd_chunked(xs, dtp, A, Bs, Cs) + d_skip.astype(jnp.float32)[:, None] * xs
    y_ssd = gated_group_rmsnorm(y.reshape(b, l, SSD_WIDTH).astype(h.dtype), z_ssd,
                                ssd_norm_w, SSD_GROUPS)
    rep = ATTN_Q_HEADS // ATTN_KV_HEADS
    qh = q.reshape(b, l, ATTN_KV_HEADS, rep, ATTN_HEAD_DIM)
    kh = k.reshape(b, l, ATTN_KV_HEADS, ATTN_HEAD_DIM)
    vh = v.reshape(b, l, ATTN_KV_HEADS, ATTN_HEAD_DIM)
    y_attn = swa_gqa_sinks(qh, kh, vh, sinks) * jax.nn.silu(z_attn)
    ca, cg = _split(conf, (CONF_WIDTH, CONF_WIDTH))
    c = ca * jax.nn.sigmoid(cg)
    c = causal_depthwise_conv(c, dw_w, dw_b)
    c = jax.nn.silu(layernorm(c, ln_w, ln_b))
    y_conf = c * jax.nn.silu(z_conf)
    return jnp.concatenate([y_ssd, y_attn, y_conf], axis=-1) @ w_out


def setup_inputs(seed: int = 0) -> dict:
    key = jax.random.key(seed)
    ks = jax.random.split(key, 20)
    f32 = jnp.float32
    nrm = lambda k, s, sc: jax.random.normal(k, s, f32) * sc
    x = jax.random.normal(ks[0], (BATCH, SEQ, D_MODEL), f32)
    norm_w = 1.0 + nrm(ks[1], (DEPTH, D_MODEL), 0.02)
    w_in = nrm(ks[2], (DEPTH, D_MODEL, D_IN_PROJ), D_MODEL ** -0.5)
    ssd_conv_w = nrm(ks[3], (DEPTH, SSD_CONV, SSD_CONV_DIM), SSD_CONV ** -0.5)
    ssd_conv_b = nrm(ks[4], (DEPTH, SSD_CONV_DIM), 0.02)
    dt0 = jnp.exp(jax.random.uniform(ks[5], (DEPTH, SSD_HEADS), f32,
                                     math.log(1e-3), math.log(1e-1)))
    ssd_dt_bias = dt0 + jnp.log(-jnp.expm1(-dt0))
    ssd_a_log = jnp.log(jax.random.uniform(ks[6], (DEPTH, SSD_HEADS), f32, 1.0, 16.0))
    ssd_d = 1.0 + nrm(ks[7], (DEPTH, SSD_HEADS), 0.1)
    ssd_norm_w = 1.0 + nrm(ks[8], (DEPTH, SSD_WIDTH), 0.02)
    attn_sinks = nrm(ks[9], (DEPTH, ATTN_Q_HEADS), 1.0)
    conf_dw_w = nrm(ks[10], (DEPTH, CONF_KERNEL, CONF_WIDTH), CONF_KERNEL ** -0.5)
    conf_dw_b = nrm(ks[11], (DEPTH, CONF_WIDTH), 0.02)
    conf_ln_w = 1.0 + nrm(ks[12], (DEPTH, CONF_WIDTH), 0.02)
    conf_ln_b = nrm(ks[13], (DEPTH, CONF_WIDTH), 0.02)
    w_out = nrm(ks[14], (DEPTH, MIX_WIDTH, D_MODEL), MIX_WIDTH ** -0.5)
    final_norm_w = 1.0 + nrm(ks[15], (D_MODEL,), 0.02)
    return {"x": x, "norm_w": norm_w, "w_in": w_in, "ssd_conv_w": ssd_conv_w,
            "ssd_conv_b": ssd_conv_b, "ssd_dt_bias": ssd_dt_bias, "ssd_a_log": ssd_a_log,
            "ssd_d": ssd_d, "ssd_norm_w": ssd_norm_w, "attn_sinks": attn_sinks,
            "conf_dw_w": conf_dw_w, "conf_dw_b": conf_dw_b, "conf_ln_w": conf_ln_w,
            "conf_ln_b": conf_ln_b, "w_out": w_out, "final_norm_w": final_norm_w}


def reference(x, norm_w, w_in, ssd_conv_w, ssd_conv_b, ssd_dt_bias, ssd_a_log, ssd_d,
              ssd_norm_w, attn_sinks, conf_dw_w, conf_dw_b, conf_ln_w, conf_ln_b, w_out,
              final_norm_w):
    for i in range(DEPTH):
        h = rmsnorm(x, norm_w[i])
        x = x + hybrid_mixer(h, w_in[i], ssd_conv_w[i], ssd_conv_b[i], ssd_dt_bias[i],
                             ssd_a_log[i], ssd_d[i], ssd_norm_w[i], attn_sinks[i],
                             conf_dw_w[i], conf_dw_b[i], conf_ln_w[i], conf_ln_b[i], w_out[i])
    return rmsnorm(x, final_norm_w)
```

```python
from contextlib import ExitStack

import numpy as np
import concourse.bass as bass
import concourse.mybir as mybir
from concourse.bass_utils import run_bass_kernel_spmd

F32 = mybir.dt.float32
BF16 = mybir.dt.bfloat16
AF = mybir.ActivationFunctionType
ALU = mybir.AluOpType

D_MODEL = 1024
SEQ = 2048
BATCH = 16
NCORES = 8
T = 512
NCH = 4
NSLOT = 21
SLOT = 4096
EPS = 1e-5
import os as _os
STOP = _os.environ.get("KSTOP", "")
NEG = -30000.0


def _region(ap):
    steps = ap.ap
    off = int(ap.offset)
    sp = str(ap.space)
    esz = mybir.dt.size(ap.dtype)
    if sp in ("SB", "PSUM"):
        pstep, pcount = steps[0]
        p0 = off // pstep
        f0 = off % pstep
        ext = sum((c - 1) * abs(s) for s, c in steps[1:])
        lo, hi = f0 * esz, (f0 + ext + 1) * esz
        if sp == "PSUM":
            lo = (lo // 2048) * 2048
            hi = ((hi + 2047) // 2048) * 2048
        return (ap.name, p0, p0 + pcount, lo, hi)
    ext = sum((c - 1) * abs(s) for s, c in steps)
    return (ap.name, 0, 1, off * esz, (off + ext + 1) * esz)


class Sched:
    SEM_LIMIT = 30000
    NDSEM = 6

    def __init__(self, nc):
        self.nc = nc
        self.ops = []
        self.track = {}
        self.untracked = set()

    def _dep(self, ap, is_write, opid, deps, eng=None):
        name, p0, p1, f0, f1 = _region(ap)
        if name in self.untracked:
            return
        recs = self.track.get(name, [])
        keep = []
        is_psum = str(ap.space) == "PSUM"
        for r in recs:
            rid, rw, a0, a1, b0, b1 = r
            ov = (a0 < p1 and p0 < a1 and b0 < f1 and f0 < b1)
            rr_conflict = (is_psum and ov and rid != opid and self.ops[rid]["eng"] != eng)
            if ov and (rw or is_write or rr_conflict) and rid != opid:
                kind = "RAW" if (rw and not is_write) else "WAX"
                deps.setdefault(rid, set()).add(kind)
            if is_write and ov and a0 >= p0 and a1 <= p1 and b0 >= f0 and b1 <= f1:
                continue
            keep.append(r)
        keep.append((opid, is_write, p0, p1, f0, f1))
        self.track[name] = keep

    def add(self, eng, fn, reads=(), writes=(), dma=False):
        opid = len(self.ops)
        deps = {}
        for ap in reads:
            if ap is not None and not isinstance(ap, (int, float)):
                self._dep(ap, False, opid, deps, eng)
        for ap in writes:
            self._dep(ap, True, opid, deps, eng)
        self.ops.append(dict(eng=eng, fn=fn, deps=deps, dma=dma))
        return opid

    def emit(self, stack):
        nc = self.nc
        engs = {"pe": nc.tensor, "act": nc.scalar, "dve": nc.vector,
                "pool": nc.gpsimd, "sp": nc.sync}
        ops = self.ops
        needed = set()
        for i, op in enumerate(ops):
            d2 = {}
            for d, kinds in op["deps"].items():
                p = ops[d]
                if p["eng"] == op["eng"] and not p["dma"] and not op["dma"]:
                    if op["eng"] == "pe":
                        continue
                d2[d] = kinds
            op["deps"] = d2
            needed.update(d2)
        semctr = [0]

        def newsem(tag):
            semctr[0] += 1
            return stack.enter_context(nc.semaphore("s_%s_%d" % (tag, semctr[0])))

        cur_sem, cur_cnt, dsems, dcount = {}, {}, {}, {}
        waited = {e: {} for e in engs}
        sig = {}
        dma_final = {}
        for i, op in enumerate(ops):
            en = op["eng"]
            eng = engs[en]
            w = waited[en]
            req = {}
            for d in op["deps"]:
                s, v = sig[d]
                key = id(s)
                if key not in req or req[key][1] < v:
                    req[key] = (s, v)
            for key, (s, v) in req.items():
                if w.get(key, 0) < v:
                    eng.wait_ge(s, v)
                    w[key] = v
            if op["dma"]:
                if en not in dsems:
                    dsems[en] = [newsem("d" + en) for _ in range(self.NDSEM)]
                    dcount[en] = 0
                k = dcount[en]
                dcount[en] = k + 1
                s = dsems[en][k % self.NDSEM]
                need = 16 * (k // self.NDSEM)
                if need > 0 and w.get(id(s), 0) < need:
                    eng.wait_ge(s, need)
                    w[id(s)] = need
                inst = op["fn"](eng)
                inst.then_inc(s, 16)
                sig[i] = (s, need + 16)
                dma_final[id(s)] = (s, need + 16)
            else:
                inst = op["fn"](eng)
                if i in needed:
                    if en not in cur_sem or cur_cnt[en] >= self.SEM_LIMIT:
                        cur_sem[en] = newsem(en)
                        cur_cnt[en] = 0
                    cur_cnt[en] += 1
                    inst.then_inc(cur_sem[en], 1)
                    sig[i] = (cur_sem[en], cur_cnt[en])
        sp = engs["sp"]
        for key, (s, v) in dma_final.items():
            if waited["sp"].get(key, 0) < v:
                sp.wait_ge(s, v)


def V(ap, dims, off=0):
    return bass.AP(ap.tensor, int(ap.offset) + off, [list(ap.ap[0])] + [list(d) for d in dims])


class Builder:
    def __init__(self, n_sc, layers, final_norm, nlayers_total=2, NR=5, dbg=None):
        self.n_sc = n_sc
        self.layers = list(layers)
        self.final_norm = final_norm
        self.NR = NR
        self.dbg = dbg or []
        nc = self.nc = bass.Bass("TRN2", target_bir_lowering=False)
        self.S = S = Sched(nc)
        ntok = n_sc * T
        L = nlayers_total
        dt_ = nc.dram_tensor
        self.x_d = dt_("x", [ntok, D_MODEL], F32, kind="ExternalInput").ap()
        self.wpack = dt_("wpack", [L, NSLOT, 128, SLOT], F32, kind="ExternalInput").ap()
        self.pvec = dt_("pvec", [L, 128, 40], F32, kind="ExternalInput").ap()
        self.bvec = dt_("bvec", [L, 64], F32, kind="ExternalInput").ap()
        self.fnw_d = dt_("fnw", [1, D_MODEL], F32, kind="ExternalInput").ap()
        self.cmat_d = dt_("cmat", [128, 768], F32, kind="ExternalInput").ap()
        self.y_d = dt_("y", [ntok, D_MODEL], F32, kind="ExternalOutput").ap()
        self.wbf = dt_("wbf", [L, NSLOT, 128, SLOT], BF16).ap()
        S.untracked |= {"x", "wpack", "pvec", "bvec", "fnw", "cmat"}
        self.dbg_out = {}
        for name, shape in self.dbg:
            self.dbg_out[name] = dt_(name, shape, F32, kind="ExternalOutput").ap()
        self._alloc()

    def sb(self, name, shape, dtype):
        return self.nc.alloc_sbuf_tensor(name, list(shape), dtype).ap()

    def _alloc(self):
        sb = self.sb
        nc = self.nc
        self.cm = sb("cm", [128, 768], F32)
        self.identb = sb("identb", [128, 128], BF16)
        self.nmb = sb("nmb", [128, 256], BF16)
        self.ugo = sb("ugo", [128, 384], BF16)
        self.ahl = sb("ahl", [128, 2, 64], BF16)
        self.fnw = sb("fnwb", [128, D_MODEL], F32)
        self.pv = {}
        self.bv = {}
        self.Sst = {}
        self.kT = {}
        self.v1 = {}
        self.c_in = {}
        self.xtail = {}
        for l in self.layers:
            self.pv[l] = sb("pv%d" % l, [128, 40], F32)
            self.bv[l] = sb("bv%d" % l, [128, 64], F32)
            self.Sst[l] = sb("S%d" % l, [128, 1024], F32)
            self.kT[l] = sb("kT%d" % l, [128, 640], BF16)
            self.v1[l] = sb("v1_%d" % l, [128, 5, 256], BF16)
            self.c_in[l] = sb("ctail%d" % l, [128, 4, 30], BF16)
            self.xtail[l] = sb("xtail%d" % l, [128, 12, 3], BF16)
        self.ring = [sb("ring%d" % i, [128, SLOT], BF16) for i in range(self.NR)]
        self.ring_cnt = 0
        self.ring_owner = {}
        self.lazyq = []
        self.cin = sb("cin", [128, 4, 542], BF16)
        self.xt = sb("xt", [128, NCH, D_MODEL], F32)
        self.hy = sb("hy", [128, 16, T], BF16)
        self.hT = sb("hT", [128, 8, T], BF16)
        self.junks = [sb("junk%d" % i, [128, 1024], BF16) for i in range(2)]
        self.junk_i = 0
        self.ss = sb("ss", [128, 8], F32)
        self.xbc_in = [sb("xbcin%d" % i, [128, 515], BF16) for i in range(3)]
        self.xbcT = sb("xbcT", [128, 12, T], BF16)
        self.qT = sb("qT", [128, 4, T], BF16)
        self.sig = sb("sig", [128, T], F32)
        self.gzc = sb("gzc", [128, 4, T], BF16)
        self.gz = sb("gz", [128, NCH, 1536], BF16)
        self.dts = sb("dts", [128, 6, 64], F32)
        self.X = sb("X", [128, 2, 2, 1024], BF16)
        self.LT = sb("LT", [128, 2048], BF16)
        self.cbTm = sb("cbTm", [128, 256], BF16)
        self.xs_tm = sb("xs_tm", [128, 1024], BF16)
        self.xdt = sb("xdt", [128, 1024], BF16)
        self.xdd = sb("xdd", [128, 1024], BF16)
        self.B_tm = sb("B_tm", [128, 256], BF16)
        self.Sbf = sb("Sbf", [128, 1024], BF16)
        self.ea = sb("ea", [128, 32], F32)
        self.ybuf = sb("ybuf", [128, 1024], F32)
        self.ytmp = sb("ytmp", [128, 1024], F32)
        self.yssd = sb("yssd", [128, 1024], BF16)
        self.gss = sb("gss", [128, 4], F32)
        self.PT = sb("PT", [128, 4, T], BF16)
        self.oa = sb("oa", [128, 512], F32)
        self.yat = sb("yat", [128, 512], BF16)
        self.den = sb("den", [128, 16], F32)
        self.cv = sb("cv", [128, 4, T], F32)
        self.sqc = [sb("sqc%d" % i, [128, 3, T], BF16) for i in range(1)]
        self.st12 = sb("st12", [128, 2 * T], F32)
        self.st1 = self.st12[:, 0:T]
        self.st2 = self.st12[:, T:2 * T]
        self.xsD = self.st12
        self.ps = nc.alloc_psum_tensor("ps", [128, 8, 512], F32).ap()
        self.psb = self.ps.bitcast(BF16)
        self.ident_f = self.cm[:, 0:128]
        self.U_f = self.cm[:, 128:256]
        self.G_f = self.cm[:, 256:384]
        self.ones_f = self.cm[:, 384:512]
        self.U_b = self.ugo[:, 0:128]
        self.G_b = self.ugo[:, 128:256]
        self.ones_b = self.ugo[:, 256:384]

    @property
    def junk(self):
        self.junk_i += 1
        return self.junks[self.junk_i % 2]

    def mm(self, out, lhsT, rhs, start=True, stop=True):
        self.S.add("pe", lambda e: e.matmul(out, lhsT=lhsT, rhs=rhs, start=start, stop=stop),
                   reads=[lhsT, rhs], writes=[out])

    def tr(self, out, in_):
        ident = self.identb
        self.S.add("pe", lambda e: e.transpose(out, in_, ident), reads=[in_, ident], writes=[out])

    def act(self, out, in_, func, bias=None, scale=None, accum=None):
        kw = {}
        rd = [in_]
        if bias is not None:
            kw["bias"] = bias
            rd.append(bias)
        if scale is not None:
            kw["scale"] = scale
            rd.append(scale)
        wr = [out]
        if accum is not None:
            kw["accum_out"] = accum
            wr.append(accum)
        self.S.add("act", lambda e: e.activation(out=out, in_=in_, func=func, **kw),
                   reads=rd, writes=wr)

    def tt(self, out, in0, in1, op, eng="dve"):
        self.S.add(eng, lambda e: e.tensor_tensor(out=out, in0=in0, in1=in1, op=op),
                   reads=[in0, in1], writes=[out])

    def tsmul(self, out, in0, scalar, eng="dve"):
        self.S.add(eng, lambda e: e.tensor_scalar_mul(out, in0, scalar),
                   reads=[in0, scalar], writes=[out])

    def ts2(self, out, in0, s1, s2, op0, op1, eng="dve"):
        self.S.add(eng, lambda e: e.tensor_scalar(out=out, in0=in0, scalar1=s1, scalar2=s2,
                                                  op0=op0, op1=op1),
                   reads=[in0, s1, s2], writes=[out])

    def stt(self, out, in0, scalar, in1, op0, op1, eng="dve"):
        self.S.add(eng, lambda e: e.scalar_tensor_tensor(out=out, in0=in0, scalar=scalar, in1=in1,
                                                         op0=op0, op1=op1),
                   reads=[in0, scalar, in1], writes=[out])

    def cp(self, out, in_, eng="dve"):
        if eng == "act":
            self.S.add("act", lambda e: e.copy(out, in_), reads=[in_], writes=[out])
        else:
            self.S.add(eng, lambda e: e.tensor_copy(out, in_), reads=[in_], writes=[out])

    def memset(self, ap, val, eng="dve"):
        self.S.add(eng, lambda e: e.memset(ap, val), reads=[], writes=[ap])

    def dma(self, out, in_, q="sp"):
        self.S.add(q, lambda e: e.dma_start(out=out, in_=in_), reads=[in_], writes=[out], dma=True)

    def load_slot(self, l, s):
        idx = self.ring_cnt % self.NR
        own = self.ring_owner.get(idx)
        assert own is None or own["users"] == 0, "ring slot still has pending users"
        r = self.ring[idx]
        self.ring_cnt += 1
        self.ring_owner[idx] = None
        self.dma(r, self.wbf[l, s], q="sp")
        return r

    def lazy_slot(self, l, s):
        h = {"ap": None, "l": l, "s": s, "users": 0}
        self.lazyq.append(h)
        return h

    def try_loads(self):
        while self.lazyq:
            idx = self.ring_cnt % self.NR
            own = self.ring_owner.get(idx)
            if own is not None and own["users"] > 0:
                break
            h = self.lazyq.pop(0)
            h["ap"] = self.load_slot(h["l"], h["s"])
            self.ring_owner[idx] = h

    def dump(self, name, ap):
        if name in self.dbg_out:
            self.dma(self.dbg_out[name], ap, q="pool")

    def cast_layer(self, l):
        for s_ in [11, 12, 0, 1, 2, 10, 7, 8, 3, 9, 5, 4, 6, 13, 14, 15, 16, 17, 18, 19, 20]:
            self.dma(self.wbf[l, s_], self.wpack[l, s_], q="pool")

    def prologue(self):
        cm = self.cm
        xv = self.x_d[0:T, :].rearrange("(j p) d -> p j d", p=128)
        self.dma(self.xt, xv, q="sp")
        self.dma(cm, self.cmat_d, q="sp")
        self.cp(self.identb, cm[:, 0:128])
        self.cp(self.nmb, cm[:, 512:768])
        self.cp(self.ugo, cm[:, 128:512])
        self.dma(self.fnw, self.fnw_d.partition_broadcast(128), q="sp")
        for l in self.layers:
            self.dma(self.pv[l], self.pvec[l], q="sp")
            self.dma(self.bv[l], self.bvec[l:l + 1, :].partition_broadcast(128), q="sp")
        self.cast_layer(self.layers[0])
        for l in self.layers:
            bv = self.bv[l]
            self.act(bv[:, 16:32], bv[:, 16:32], AF.Exp)
            self.tsmul(bv[:, 16:32], bv[:, 16:32], -1.0)
            self.act(bv[:, 48:56], bv[:, 48:56], AF.Exp)
            v1 = self.v1[l]
            self.memset(v1, 1.0)

    def layer_pass(self, sc, l):
        seq_start = (sc % (SEQ // T) == 0)
        ps, psb = self.ps, self.psb
        pv, bv = self.pv[l], self.bv[l]
        xt = self.xt
        hy = self.hy
        h = V(hy, [[1024, 4], [1, 1024]])
        hT = self.hT
        yT = hy
        ss = self.ss

        if seq_start:
            self.memset(self.xtail[l], 0.0)
            self.memset(self.c_in[l], 0.0)
        else:
            self.cp(self.Sbf, self.Sst[l], eng="act")

        abank = (0, 2, 6, 4)
        nw8 = V(pv[:, 0:1], [[1, 8], [0, 128]])
        for j in range(NCH):
            cs = slice(j * 128, (j + 1) * 128)
            self.act(self.junk, xt[:, j, :], AF.Square, accum=ss[:, j:j + 1])
            self.act(ss[:, 4 + j:5 + j], ss[:, j:j + 1], AF.Ln, scale=1.0 / D_MODEL, bias=EPS)
            self.act(ss[:, 4 + j:5 + j], ss[:, 4 + j:5 + j], AF.Exp, scale=-0.5)
            self.tsmul(h[:, j, :], xt[:, j, :], ss[:, 4 + j:5 + j])
            for kc in range(8):
                self.tr(psb[:, abank[j], kc * 128:(kc + 1) * 128], h[:, j, kc * 128:(kc + 1) * 128])
            self.tt(hT[:, :, cs], V(psb[:, abank[j], 0:1], [[128, 8], [1, 128]]), nw8, ALU.mult)
        if STOP == "A":
            return
        bank_rot = [0]

        def nbank(lo, n):
            b = lo + bank_rot[0] % n
            bank_rot[0] += 1
            return b

        def fm_block(slot, c0, bank):
            s3 = V(slot, [[512, 8], [1, 512]])
            for kc in range(8):
                self.mm(ps[:, bank, :], s3[:, kc, c0:c0 + 128], hT[:, kc, :],
                        start=(kc == 0), stop=(kc == 7))

        d4 = [self.load_slot(l, 11), self.load_slot(l, 12)]
        xtail = self.xtail[l]
        xslots = {}

        def xbc_front(b):
            g, sub = divmod(b, 4)
            if g not in xslots:
                xslots[g] = self.load_slot(l, g)
            bank = nbank(0, 4)
            fm_block(xslots[g], sub * 128, bank)
            xin = self.xbc_in[b % 3]
            self.cp(xin[:, 0:3], xtail[:, b, :], eng="dve")
            self.cp(xin[:, 3:515], ps[:, bank, :], eng="act")
            self.cp(xtail[:, b, :], xin[:, 512:515], eng="dve")

        def xbc_conv(b):
            xin = self.xbc_in[b % 3]
            cbank = 4 + (b % 2)
            dsl = d4[b // 6]
            bb = b % 6
            for k in range(4):
                self.mm(ps[:, cbank, :], dsl[:, (bb * 4 + k) * 128:(bb * 4 + k + 1) * 128],
                        xin[:, k:k + T], start=(k == 0), stop=(k == 3))
            self.act(self.xbcT[:, b, :], ps[:, cbank, :], AF.Silu, bias=pv[:, 16 + b:17 + b])

        xbc_front(0)
        for b in range(12):
            if b + 1 < 12:
                xbc_front(b + 1)
            xbc_conv(b)
        slot = self.load_slot(l, 10)
        s3 = V(slot, [[512, 8], [1, 512]])
        kT, v1 = self.kT[l], self.v1[l]
        bank = nbank(0, 4)
        fm_block(slot, 0, bank)
        self.cp(kT[:, 128:640], ps[:, bank, :], eng="act")
        dts = self.dts
        for j in range(NCH):
            vb = 6 + (j % 2)
            for kc in range(8):
                self.mm(ps[:, vb, 0:144], hT[:, kc, j * 128:(j + 1) * 128], s3[:, kc, 128:272],
                        start=(kc == 0), stop=(kc == 7))
            self.cp(V(v1[:, 1 + j, 0:1], [[128, 2], [1, 64]]), V(ps[:, vb, 0:1], [[64, 2], [1, 64]]),
                    eng="dve")
            self.cp(dts[:, 0, j * 16:(j + 1) * 16], ps[:, vb, 128:144], eng="dve")
        for g3 in range(2):
            slot = self.load_slot(l, 7 + g3)
            s3 = V(slot, [[512, 8], [1, 512]])
            for j in range(NCH):
                bank = nbank(0, 4)
                for kc in range(8):
                    self.mm(ps[:, bank, :], hT[:, kc, j * 128:(j + 1) * 128], s3[:, kc, :],
                            start=(kc == 0), stop=(kc == 7))
                self.act(self.gz[:, j, g3 * 512:(g3 + 1) * 512], ps[:, bank, :], AF.Silu)
        dtraw, dtx, dab, de, dtt, da = [dts[:, i, :] for i in range(6)]
        b16 = lambda c0: V(bv[:, c0:c0 + 1], [[0, 4], [1, 16]])
        v416 = lambda a: V(a, [[16, 4], [1, 16]])
        self.tt(v416(dtx), v416(dtraw), b16(0), ALU.add)
        self.tsmul(dab, dtx, -1.0)
        self.tt(dab, dab, dtx, ALU.min)
        self.act(de, dab, AF.Exp)
        self.act(de, de, AF.Ln, bias=1.0)
        self.stt(dtt, dtx, 0.0, de, ALU.max, ALU.add)
        self.tt(v416(da), v416(dtt), b16(16), ALU.mult)
        self.cp(self.ahl[:, 0, :], da, eng="dve")
        self.tt(self.ahl[:, 1, :], da, self.ahl[:, 0, :], ALU.subtract)
        self.build_X(0)

        if sc == 0 and l == self.layers[0]:
            for l2 in self.layers[1:]:
                self.cast_layer(l2)
        c_in = self.cin
        cv = self.cv
        self.cp(c_in[:, :, 0:30], self.c_in[l], eng="dve")
        fq = self.fq = []
        self.f_inflight = []
        self.f_free = [0, 1]
        sl_q = self.lazy_slot(l, 3)
        sl_za = self.lazy_slot(l, 9)
        sl_g = self.lazy_slot(l, 5)
        sl_a = self.lazy_slot(l, 4)
        sl_zc = self.lazy_slot(l, 6)

        def u_fm(tag, slot, c0, evac):
            slot["users"] += 1

            def mmf(bank):
                fm_block(slot["ap"], c0, bank)
                slot["users"] -= 1
            fq.append((tag, 8, mmf, evac))

        def u_za(j):
            sl_za["users"] += 1

            def mmf(bank):
                s3z = V(sl_za["ap"], [[512, 8], [1, 512]])
                for kc in range(8):
                    self.mm(ps[:, bank, :], hT[:, kc, j * 128:(j + 1) * 128], s3z[:, kc, :],
                            start=(kc == 0), stop=(kc == 7))
                sl_za["users"] -= 1
            fq.append(("za%d" % j, 8, mmf,
                       lambda bank: self.act(self.gz[:, j, 1024:1536], ps[:, bank, :], AF.Silu)))

        def u_glu(b):
            u_fm("cg%d" % b, sl_g, b * 128,
                 lambda bank: self.act(self.sig, ps[:, bank, :], AF.Tanh, scale=0.5))
            u_fm("ca%d" % b, sl_a, b * 128,
                 lambda bank: self.stt(c_in[:, b, 30:542], self.sig, 1.0, ps[:, bank, :],
                                       ALU.add, ALU.mult))

        for b in range(4):
            u_fm("q%d" % b, sl_q, b * 128,
                 (lambda bank, b=b: self.cp(self.qT[:, b, :], ps[:, bank, :], eng="act")))
        u_za(0)
        u_glu(0)
        u_glu(1)
        u_za(1)
        u_glu(2)
        u_glu(3)
        u_za(2)
        for b in range(4):
            u_fm("zc%d" % b, sl_zc, b * 128,
                 (lambda bank, b=b: self.act(self.gzc[:, b, :], ps[:, bank, :], AF.Silu)))
        u_za(3)
        d31 = {}

        def u_conv(b):
            d31[b] = self.lazy_slot(l, 13 + b)
            d31[b]["users"] += 1

            def mmf(bank):
                for k in range(31):
                    self.mm(ps[:, bank, :], d31[b]["ap"][:, k * 128:(k + 1) * 128], c_in[:, b, k:k + T],
                            start=(k == 0), stop=(k == 30))
                d31[b]["users"] -= 1
            fq.append(("cv%d" % b, 31, mmf,
                       lambda bank: self.act(cv[:, b, :], ps[:, bank, :], AF.Identity, scale=0.5,
                                             bias=pv[:, 28 + b:29 + b])))
        for b in range(4):
            u_conv(b)
        self.try_loads()

        if STOP == "B":
            self.fill(10 ** 6)
            self.fill(10 ** 6)
            return
        for j in range(NCH):
            self.ssd_chunk(l, j, first=(seq_start and j == 0))
            self.ensure(["q0", "q1", "q2", "q3", "za%d" % j])
            self.attn_chunk(l, j, first=(seq_start and j == 0))
        while self.fq or self.f_inflight:
            self.fill(10 ** 6)
        self.cp(kT[:, 0:128], kT[:, 512:640], eng="dve")
        self.cp(v1[:, 0, :], v1[:, 4, :], eng="dve")

        wo = [self.load_slot(l, 17 + q) for q in range(4)]
        obank = {0: (0, 1), 1: (2, 3), 2: (6, 7), 3: (4, 5)}

        def oproj(js, qs):
            for q in qs:
                w3 = V(wo[q], [[1024, 4], [1, 1024]])
                for j in js:
                    for half in range(2):
                        for mi in range(4):
                            mb = q * 4 + mi
                            self.mm(ps[:, obank[j][half], :], yT[:, mb, j * 128:(j + 1) * 128],
                                    w3[:, mi, half * 512:(half + 1) * 512],
                                    start=(mb == 0), stop=(mb == 15))

        oproj([0, 1], [0, 1, 2])
        for b in range(4):
            sq = self.sqc[0]
            self.act(sq[:, 0, :], cv[:, b, :], AF.Square)
            self.cp(sq[:, 1, :], cv[:, b, :], eng="dve")
            self.tt(sq[:, 2, :], cv[:, b, :], sq[:, 1, :], ALU.subtract)
            self.mm(ps[:, 4, :], self.ones_b, sq[:, 1, :], start=(b == 0), stop=False)
            self.mm(ps[:, 4, :], self.ones_b, sq[:, 2, :], start=False, stop=(b == 3))
            self.mm(ps[:, 5, :], self.ones_b, sq[:, 0, :], start=(b == 0), stop=(b == 3))
        self.cp(self.c_in[l], c_in[:, :, 512:542], eng="dve")
        st1, st2 = self.st1, self.st2
        self.tsmul(st1, ps[:, 4, :], 1.0 / 512)
        self.tt(st2, st1, st1, ALU.mult)
        self.stt(st2, ps[:, 5, :], 1.0 / 512, st2, ALU.mult, ALU.subtract)
        oproj([2, 3], [0, 1, 2])
        self.act(st2, st2, AF.Ln, bias=EPS)
        self.act(st2, st2, AF.Exp, scale=-0.5)
        for b in range(4):
            self.tt(cv[:, b, :], cv[:, b, :], st1, ALU.subtract)
            self.tt(cv[:, b, :], cv[:, b, :], st2, ALU.mult)
            self.act(cv[:, b, :], cv[:, b, :], AF.Silu, scale=pv[:, 32 + b:33 + b],
                     bias=pv[:, 36 + b:37 + b])
            self.tt(yT[:, 12 + b, :], cv[:, b, :], self.gzc[:, b, :], ALU.mult)
        oproj([0, 1, 2, 3], [3])
        for j in range(NCH):
            for half in range(2):
                xs = xt[:, j, half * 512:(half + 1) * 512]
                self.tt(xs, ps[:, obank[j][half], :], xs, ALU.add)

    def build_X(self, j):
        Ub8 = V(self.U_b, [[0, 8], [1, 128]])
        for hh in range(2):
            for hl in range(2):
                ab = V(self.ahl[:, hl, j * 16 + hh * 8:j * 16 + hh * 8 + 1], [[1, 8], [0, 128]])
                self.tt(V(self.X[:, hh, hl, 0:1], [[128, 8], [1, 128]]), ab, Ub8, ALU.mult, eng="dve")

    def fill(self, budget):
        for evac, bank in self.f_inflight:
            evac(bank)
            self.f_free.append(bank)
        self.f_inflight = []
        self.try_loads()
        while budget > 0 and self.fq and self.f_free:
            tag, cost, mmf, evac = self.fq.pop(0)
            bank = self.f_free.pop(0)
            mmf(bank)
            self.f_inflight.append((evac, bank))
            budget -= cost
            self.try_loads()

    def ensure(self, tags):
        while any(t[0] in tags for t in self.fq):
            self.fill(8)
        self.fill(0)

    def ssd_chunk(self, l, j, first):
        ps, psb = self.ps, self.psb
        pv, bv = self.pv[l], self.bv[l]
        xbcT = self.xbcT
        cs = slice(j * 128, (j + 1) * 128)
        dts = self.dts
        dt_j = dts[:, 4, j * 16:(j + 1) * 16]
        a_j = dts[:, 5, j * 16:(j + 1) * 16]
        Sst = self.Sst[l]
        yT = self.hy
        for b in range(8):
            self.tr(psb[:, 7, b * 128:(b + 1) * 128], xbcT[:, b, cs])
        for g in range(2):
            self.tr(psb[:, 6, 512 + g * 128:512 + (g + 1) * 128], xbcT[:, 8 + g, cs])
        self.cp(self.xs_tm, psb[:, 7, :], eng="act")
        dtb = V(dt_j, [[1, 16], [0, 64]])
        self.tt(V(self.xdt, [[64, 16], [1, 64]]), V(self.xs_tm, [[64, 16], [1, 64]]), dtb, ALU.mult)
        self.cp(self.B_tm, psb[:, 6, 512:768], eng="act")
        Db = V(bv[:, 32:33], [[1, 16], [0, 64]])
        if STOP == "E1":
            return
        for g in range(2):
            self.mm(ps[:, 6, g * 128:(g + 1) * 128], xbcT[:, 8 + g, cs], xbcT[:, 10 + g, cs])
        Ub = V(self.U_b, [[0, 2], [1, 128]])
        for hl in range(2):
            a_hl = self.ahl[:, hl, j * 16:(j + 1) * 16]
            self.mm(ps[:, 6, 384:400], self.U_b, a_hl, start=(hl == 0), stop=(hl == 1))
        for hl in range(2):
            a_hl = self.ahl[:, hl, j * 16:(j + 1) * 16]
            self.mm(ps[:, 6, 400:416], self.ones_b, a_hl, start=(hl == 0), stop=(hl == 1))
        self.tt(V(self.cbTm, [[128, 2], [1, 128]]), V(ps[:, 6, 0:1], [[128, 2], [1, 128]]), Ub, ALU.mult)
        self.act(self.ea, ps[:, 6, 384:416], AF.Exp)
        if not first:
            cdb0 = V(self.ea[:, 16:17], [[1, 16], [0, 64]])
            S30 = V(Sst, [[64, 16], [1, 64]])
            xd0 = self.xdt
            self.S.add("pool", lambda e: e.tensor_tensor(out=S30, in0=S30, in1=cdb0, op=ALU.mult),
                       reads=[S30, cdb0, xd0], writes=[S30])
        if STOP == "E2":
            return
        for hh in range(2):
            for q in range(2):
                for hl in range(2):
                    self.mm(ps[:, 2 + 2 * hh + q, :], self.G_b, self.X[:, hh, hl, q * 512:(q + 1) * 512],
                            start=(hl == 0), stop=(hl == 1))
            self.act(self.LT[:, hh * 1024:(hh + 1) * 1024],
                     V(ps[:, 2 + 2 * hh, 0:1], [[1, 1024]]), AF.Exp)
        self.fill(20)
        dte = V(self.LT[:, 127:128], [[128, 16], [0, 64]])
        self.tt(V(self.xdd, [[64, 16], [1, 64]]), V(self.xdt, [[64, 16], [1, 64]]), dte, ALU.mult)
        LT4 = V(self.LT, [[1024, 2], [128, 8], [1, 128]])
        cb4 = V(self.cbTm, [[128, 2], [0, 8], [1, 128]])
        self.tt(LT4, LT4, cb4, ALU.mult)
        Db = V(bv[:, 32:33], [[1, 16], [0, 64]])
        xsD_o, xs_i, mt_dep = V(self.xsD, [[64, 16], [1, 64]]), V(self.xs_tm, [[64, 16], [1, 64]]), self.LT
        self.S.add("pool", lambda e: e.tensor_tensor(out=xsD_o, in0=xs_i, in1=Db, op=ALU.mult),
                   reads=[xs_i, Db, mt_dep], writes=[xsD_o])
        if STOP == "E4":
            return
        for hd in range(16):
            self.mm(ps[:, 6 + hd // 8, (hd % 8) * 64:(hd % 8 + 1) * 64],
                    self.LT[:, hd * 128:(hd + 1) * 128], self.xdt[:, hd * 64:(hd + 1) * 64])
        if not first:
            for g in range(2):
                self.mm(ps[:, 2 + g, :], xbcT[:, 10 + g, cs], self.Sbf[:, g * 512:(g + 1) * 512])
        for g in range(2):
            self.mm(ps[:, 4 + g, :], self.B_tm[:, g * 128:(g + 1) * 128],
                    self.xdd[:, g * 512:(g + 1) * 512])
        self.fill(40)
        ybuf, ytmp = self.ybuf, self.ytmp
        yd = V(ps[:, 6, 0:1], [[1, 1024]])
        if not first:
            eab = V(self.ea[:, 0:1], [[1, 16], [0, 64]])
            self.tt(V(ytmp, [[64, 16], [1, 64]]), V(ps[:, 2, 0:1], [[64, 16], [1, 64]]), eab, ALU.mult)
            self.tt(ybuf, yd, ytmp, ALU.add)
        else:
            self.cp(ybuf, yd, eng="dve")
        self.tt(ybuf, ybuf, self.xsD, ALU.add)
        snew = V(ps[:, 4, 0:1], [[1, 1024]])
        if not first:
            self.tt(Sst, Sst, snew, ALU.add)
        else:
            self.cp(Sst, snew, eng="dve")
        self.cp(self.Sbf, Sst, eng="act")
        if STOP == "E6":
            return
        self.tt(ybuf, ybuf, self.gz[:, j, 0:1024], ALU.mult)
        if j + 1 < NCH:
            self.build_X(j + 1)
        gss = self.gss
        for g in range(2):
            self.act(self.junk[:, 0:512], ybuf[:, g * 512:(g + 1) * 512], AF.Square,
                     accum=gss[:, g:g + 1])
        self.act(gss[:, 2:4], gss[:, 0:2], AF.Ln, scale=1.0 / 512, bias=EPS)
        self.act(gss[:, 2:4], gss[:, 2:4], AF.Exp, scale=-0.5)
        for g in range(2):
            self.tsmul(self.yssd[:, g * 512:(g + 1) * 512], ybuf[:, g * 512:(g + 1) * 512],
                       gss[:, 2 + g:3 + g])
        for b in range(8):
            self.tr(psb[:, 2, b * 128:(b + 1) * 128], self.yssd[:, b * 128:(b + 1) * 128])
        snw = V(pv[:, 8:9], [[1, 8], [0, 128]])
        self.tt(yT[:, 0:8, cs], V(psb[:, 2, 0:1], [[128, 8], [1, 128]]), snw, ALU.mult)

    def attn_chunk(self, l, j, first):
        ps, psb = self.ps, self.psb
        bv = self.bv[l]
        kT, v1 = self.kT[l], self.v1[l]
        qT = self.qT
        yT = self.hy
        cs = slice(j * 128, (j + 1) * 128)
        kbs = [1] if first else [0, 1]
        for g in range(2):
            prt = slice(g * 64, (g + 1) * 64)
            for kb in kbs:
                bank = 2 + g * 2 + kb
                kcols = slice((j + kb) * 128, (j + kb + 1) * 128)
                self.mm(ps[:, bank, :], kT[prt, kcols], V(qT[prt, 0, cs], [[T, 4], [1, 128]]),
                        start=True, stop=False)
                nm = V(self.nmb[:, (1 - kb) * 128:(1 - kb) * 128 + 1], [[0, 4], [1, 128]])
                self.mm(ps[:, bank, :], self.identb, nm, start=False, stop=True)
                self.act(self.PT[:, bank - 2, :], ps[:, bank, :], AF.Exp, scale=0.125)
        self.fill(12)
        for g in range(2):
            for r in range(4):
                hd = 4 * g + r
                o = ps[:, 6 + hd // 4, (hd % 4) * 128:(hd % 4) * 128 + 66]
                for n, kb in enumerate(kbs):
                    self.mm(o, self.PT[:, g * 2 + kb, r * 128:(r + 1) * 128],
                            v1[:, j + kb, g * 128:g * 128 + 66],
                            start=(n == 0), stop=(n == len(kbs) - 1))
        self.fill(8)
        o4 = V(ps[:, 6, 0:1], [[512, 2], [128, 4], [1, 64]])
        osum = V(ps[:, 6, 64:65], [[512, 2], [128, 4]])
        den = self.den
        self.tt(V(den[:, 0:1], [[4, 2], [1, 4]]), osum, V(bv[:, 48:49], [[4, 2], [1, 4]]), ALU.add)
        self.S.add("dve", lambda e: e.reciprocal(den[:, 8:16], den[:, 0:8]),
                   reads=[den[:, 0:8]], writes=[den[:, 8:16]])
        rb = V(den[:, 8:9], [[4, 2], [1, 4], [0, 64]])
        self.tt(V(self.oa, [[256, 2], [64, 4], [1, 64]]), o4, rb, ALU.mult)
        self.tt(self.yat, self.oa, self.gz[:, j, 1024:1536], ALU.mult)
        for b in range(4):
            self.tr(psb[:, 2, b * 128:(b + 1) * 128], self.yat[:, b * 128:(b + 1) * 128])
        self.cp(yT[:, 8:12, cs], V(psb[:, 2, 0:1], [[128, 4], [1, 128]]), eng="act")

    def finish_sc(self, sc):
        xt, ss = self.xt, self.ss
        xos = [self.ybuf, self.ytmp]
        for j in range(NCH):
            rows = self.y_d[sc * T + j * 128: sc * T + (j + 1) * 128, :]
            if self.final_norm:
                self.act(self.junk, xt[:, j, :], AF.Square, accum=ss[:, j:j + 1])
                self.act(ss[:, 4 + j:5 + j], ss[:, j:j + 1], AF.Ln, scale=1.0 / D_MODEL, bias=EPS)
                self.act(ss[:, 4 + j:5 + j], ss[:, 4 + j:5 + j], AF.Exp, scale=-0.5)
                xo = xos[j % 2]
                self.stt(xo, xt[:, j, :], ss[:, 4 + j:5 + j], self.fnw, ALU.mult, ALU.mult)
                self.dma(rows, xo, q="pool")
            else:
                self.dma(rows, xt[:, j, :], q="pool")
            if sc + 1 < self.n_sc:
                nrows = self.x_d[(sc + 1) * T + j * 128:(sc + 1) * T + (j + 1) * 128, :]
                self.dma(xt[:, j, :], nrows, q="pool")

    def build(self):
        self.prologue()
        for sc in range(self.n_sc):
            xv = self.x_d[sc * T:(sc + 1) * T, :].rearrange("(j p) d -> p j d", p=128)
            for l in self.layers:
                self.layer_pass(sc, l)
            self.finish_sc(sc)
        self.stack = ExitStack()
        self.S.emit(self.stack)
        print("sbuf bytes remaining", self.nc.sbuf_bytes_remaining, "ops", len(self.S.ops))
        return self.nc


def _consts():
    cm = np.zeros((128, 768), np.float32)
    idx = np.arange(128)
    cm[:, 0:128] = np.eye(128, dtype=np.float32)
    cm[:, 128:256] = (idx[:, None] <= idx[None, :])
    cm[:, 256:384] = (idx[:, None] > idx[None, :])
    cm[:, 384:512] = 1.0
    cm[:, 512:640] = np.where(idx[:, None] > idx[None, :], NEG, 0.0)
    cm[:, 640:768] = np.where(idx[:, None] <= idx[None, :], NEG, 0.0)
    return cm


def pack_weights(w_in, w_out, ssd_conv_w, conf_dw_w):
    L = w_in.shape[0]
    wp = np.zeros((L, NSLOT, 128, SLOT), np.float32)
    zc, xc, dtc, qc, kc_, vc, cc = 0, 2048, 3584, 3600, 4112, 4240, 4368
    qperm = []
    for b in range(4):
        qperm += list(range(qc + b * 64, qc + (b + 1) * 64))
        qperm += list(range(qc + (4 + b) * 64, qc + (5 + b) * 64))
    groups = [
        list(range(xc, xc + 512)), list(range(xc + 512, xc + 1024)), list(range(xc + 1024, xc + 1536)),
        qperm,
        list(range(cc, cc + 512)), list(range(cc + 512, cc + 1024)),
        list(range(zc + 1536, zc + 2048)),
        list(range(zc, zc + 512)), list(range(zc + 512, zc + 1024)), list(range(zc + 1024, zc + 1536)),
        list(range(kc_, kc_ + 128)) + list(range(vc, vc + 128)) + list(range(dtc, dtc + 16)),
    ]
    for l in range(L):
        wl = w_in[l].reshape(8, 128, -1)
        for g, cols in enumerate(groups):
            blk = np.zeros((128, 8, 512), np.float32)
            blk[:, :, :len(cols)] = wl[:, :, cols].transpose(1, 0, 2)
            wp[l, g] = blk.reshape(128, SLOT)
        cw = ssd_conv_w[l]
        for b in range(12):
            s, bb = divmod(b, 6)
            d = wp[l, 11 + s].reshape(128, 32, 128)
            for k in range(4):
                d[np.arange(128), bb * 4 + k, np.arange(128)] = cw[k, b * 128:(b + 1) * 128]
        dw = conf_dw_w[l]
        for b in range(4):
            d = wp[l, 13 + b].reshape(128, 32, 128)
            for k in range(31):
                d[np.arange(128), k, np.arange(128)] = dw[k, b * 128:(b + 1) * 128]
        wo = w_out[l].reshape(16, 128, 1024)
        for q in range(4):
            wp[l, 17 + q] = wo[4 * q:4 * q + 4].transpose(1, 0, 2).reshape(128, SLOT)
    return wp


def pack_vecs(norm_w, ssd_norm_w, ssd_conv_b, conf_dw_b, conf_ln_w, conf_ln_b,
              ssd_dt_bias, ssd_a_log, ssd_d, attn_sinks):
    L = norm_w.shape[0]
    pvec = np.zeros((L, 128, 40), np.float32)
    bvec = np.zeros((L, 64), np.float32)
    for l in range(L):
        pvec[l, :, 0:8] = norm_w[l].reshape(8, 128).T
        pvec[l, :, 8:16] = ssd_norm_w[l].reshape(8, 128).T
        pvec[l, :, 16:28] = ssd_conv_b[l].reshape(12, 128).T
        pvec[l, :, 28:32] = conf_dw_b[l].reshape(4, 128).T
        pvec[l, :, 32:36] = conf_ln_w[l].reshape(4, 128).T
        pvec[l, :, 36:40] = conf_ln_b[l].reshape(4, 128).T
        bvec[l, 0:16] = ssd_dt_bias[l]
        bvec[l, 16:32] = ssd_a_log[l]
        bvec[l, 32:48] = ssd_d[l]
        bvec[l, 48:56] = attn_sinks[l]
    return pvec, bvec


_PROG_CACHE = {}


def get_program(n_sc, layers, final_norm):
    key = (n_sc, tuple(layers), final_norm)
    if key not in _PROG_CACHE:
        b = Builder(n_sc, layers, final_norm)
        b.build()
        _PROG_CACHE[key] = b
    return _PROG_CACHE[key]


def kernel(x, norm_w, w_in, ssd_conv_w, ssd_conv_b, ssd_dt_bias, ssd_a_log, ssd_d,
           ssd_norm_w, attn_sinks, conf_dw_w, conf_dw_b, conf_ln_w, conf_ln_b, w_out,
           final_norm_w):
    x = np.ascontiguousarray(np.asarray(x, dtype=np.float32))
    args = [np.asarray(a, dtype=np.float32) for a in
            (norm_w, w_in, ssd_conv_w, ssd_conv_b, ssd_dt_bias, ssd_a_log, ssd_d, ssd_norm_w,
             attn_sinks, conf_dw_w, conf_dw_b, conf_ln_w, conf_ln_b, w_out, final_norm_w)]
    (norm_w, w_in, ssd_conv_w, ssd_conv_b, ssd_dt_bias, ssd_a_log, ssd_d, ssd_norm_w,
     attn_sinks, conf_dw_w, conf_dw_b, conf_ln_w, conf_ln_b, w_out, final_norm_w) = args
    wp = pack_weights(w_in, w_out, ssd_conv_w, conf_dw_w)
    pvec, bvec = pack_vecs(norm_w, ssd_norm_w, ssd_conv_b, conf_dw_b, conf_ln_w, conf_ln_b,
                           ssd_dt_bias, ssd_a_log, ssd_d, attn_sinks)
    cm = _consts()
    fnw = final_norm_w.reshape(1, D_MODEL)
    seq_per_core = BATCH // NCORES
    n_sc = seq_per_core * SEQ // T
    prog = get_program(n_sc, (0, 1), True)
    in_maps = []
    for c in range(NCORES):
        xs = x[c * seq_per_core:(c + 1) * seq_per_core].reshape(-1, D_MODEL)
        in_maps.append({"x": xs, "wpack": wp, "pvec": pvec, "bvec": bvec, "fnw": fnw, "cmat": cm})
    res = run_bass_kernel_spmd(prog.nc, in_maps, core_ids=list(range(NCORES)))
    out = np.concatenate([r["y"].reshape(seq_per_core, SEQ, D_MODEL) for r in res.results], axis=0)
    return out.astype(np.float32)
```

```python
from contextlib import ExitStack

import numpy as np
import concourse.bass as bass
import concourse.mybir as mybir
from concourse.bass_utils import run_bass_kernel_spmd

F32 = mybir.dt.float32
BF16 = mybir.dt.bfloat16
AF = mybir.ActivationFunctionType
ALU = mybir.AluOpType

D_MODEL = 1024
SEQ = 2048
BATCH = 16
NCORES = 8
T = 512
NCH = 4
NSLOT = 21
SLOT = 4096
EPS = 1e-5
import os as _os
STOP = _os.environ.get("KSTOP", "")
NEG = -30000.0


def _region(ap):
    steps = ap.ap
    off = int(ap.offset)
    sp = str(ap.space)
    esz = mybir.dt.size(ap.dtype)
    if sp in ("SB", "PSUM"):
        pstep, pcount = steps[0]
        p0 = off // pstep
        f0 = off % pstep
        ext = sum((c - 1) * abs(s) for s, c in steps[1:])
        lo, hi = f0 * esz, (f0 + ext + 1) * esz
        if sp == "PSUM":
            lo = (lo // 2048) * 2048
            hi = ((hi + 2047) // 2048) * 2048
        return (ap.name, p0, p0 + pcount, lo, hi)
    ext = sum((c - 1) * abs(s) for s, c in steps)
    return (ap.name, 0, 1, off * esz, (off + ext + 1) * esz)


class Sched:
    SEM_LIMIT = 30000
    NDSEM = 6

    def __init__(self, nc):
        self.nc = nc
        self.ops = []
        self.track = {}
        self.untracked = set()

    def _dep(self, ap, is_write, opid, deps, eng=None):
        name, p0, p1, f0, f1 = _region(ap)
        if name in self.untracked:
            return
        recs = self.track.get(name, [])
        keep = []
        is_psum = str(ap.space) == "PSUM"
        for r in recs:
            rid, rw, a0, a1, b0, b1 = r
            ov = (a0 < p1 and p0 < a1 and b0 < f1 and f0 < b1)
            rr_conflict = (is_psum and ov and rid != opid and self.ops[rid]["eng"] != eng)
            if ov and (rw or is_write or rr_conflict) and rid != opid:
                kind = "RAW" if (rw and not is_write) else "WAX"
                deps.setdefault(rid, set()).add(kind)
            if is_write and ov and a0 >= p0 and a1 <= p1 and b0 >= f0 and b1 <= f1:
                continue
            keep.append(r)
        keep.append((opid, is_write, p0, p1, f0, f1))
        self.track[name] = keep

    def add(self, eng, fn, reads=(), writes=(), dma=False):
        opid = len(self.ops)
        deps = {}
        for ap in reads:
            if ap is not None and not isinstance(ap, (int, float)):
                self._dep(ap, False, opid, deps, eng)
        for ap in writes:
            self._dep(ap, True, opid, deps, eng)
        self.ops.append(dict(eng=eng, fn=fn, deps=deps, dma=dma))
        return opid

    def emit(self, stack):
        nc = self.nc
        engs = {"pe": nc.tensor, "act": nc.scalar, "dve": nc.vector,
                "pool": nc.gpsimd, "sp": nc.sync}
        ops = self.ops
        needed = set()
        for i, op in enumerate(ops):
            d2 = {}
            for d, kinds in op["deps"].items():
                p = ops[d]
                if p["eng"] == op["eng"] and not p["dma"] and not op["dma"]:
                    if op["eng"] == "pe":
                        continue
                d2[d] = kinds
            op["deps"] = d2
            needed.update(d2)
        semctr = [0]

        def newsem(tag):
            semctr[0] += 1
            return stack.enter_context(nc.semaphore("s_%s_%d" % (tag, semctr[0])))

        cur_sem, cur_cnt, dsems, dcount = {}, {}, {}, {}
        waited = {e: {} for e in engs}
        sig = {}
        dma_final = {}
        for i, op in enumerate(ops):
            en = op["eng"]
            eng = engs[en]
            w = waited[en]
            req = {}
            for d in op["deps"]:
                s, v = sig[d]
                key = id(s)
                if key not in req or req[key][1] < v:
                    req[key] = (s, v)
            for key, (s, v) in req.items():
                if w.get(key, 0) < v:
                    eng.wait_ge(s, v)
                    w[key] = v
            if op["dma"]:
                if en not in dsems:
                    dsems[en] = [newsem("d" + en) for _ in range(self.NDSEM)]
                    dcount[en] = 0
                k = dcount[en]
                dcount[en] = k + 1
                s = dsems[en][k % self.NDSEM]
                need = 16 * (k // self.NDSEM)
                if need > 0 and w.get(id(s), 0) < need:
                    eng.wait_ge(s, need)
                    w[id(s)] = need
                inst = op["fn"](eng)
                inst.then_inc(s, 16)
                sig[i] = (s, need + 16)
                dma_final[id(s)] = (s, need + 16)
            else:
                inst = op["fn"](eng)
                if i in needed:
                    if en not in cur_sem or cur_cnt[en] >= self.SEM_LIMIT:
                        cur_sem[en] = newsem(en)
                        cur_cnt[en] = 0
                    cur_cnt[en] += 1
                    inst.then_inc(cur_sem[en], 1)
                    sig[i] = (cur_sem[en], cur_cnt[en])
        sp = engs["sp"]
        for key, (s, v) in dma_final.items():
            if waited["sp"].get(key, 0) < v:
                sp.wait_ge(s, v)


def V(ap, dims, off=0):
    return bass.AP(ap.tensor, int(ap.offset) + off, [list(ap.ap[0])] + [list(d) for d in dims])


class Builder:
    def __init__(self, n_sc, layers, final_norm, nlayers_total=2, NR=5, dbg=None):
        self.n_sc = n_sc
        self.layers = list(layers)
        self.final_norm = final_norm
        self.NR = NR
        self.dbg = dbg or []
        nc = self.nc = bass.Bass("TRN2", target_bir_lowering=False)
        self.S = S = Sched(nc)
        ntok = n_sc * T
        L = nlayers_total
        dt_ = nc.dram_tensor
        self.x_d = dt_("x", [ntok, D_MODEL], F32, kind="ExternalInput").ap()
        self.wpack = dt_("wpack", [L, NSLOT, 128, SLOT], F32, kind="ExternalInput").ap()
        self.pvec = dt_("pvec", [L, 128, 40], F32, kind="ExternalInput").ap()
        self.bvec = dt_("bvec", [L, 64], F32, kind="ExternalInput").ap()
        self.fnw_d = dt_("fnw", [1, D_MODEL], F32, kind="ExternalInput").ap()
        self.cmat_d = dt_("cmat", [128, 768], F32, kind="ExternalInput").ap()
        self.y_d = dt_("y", [ntok, D_MODEL], F32, kind="ExternalOutput").ap()
        self.wbf = dt_("wbf", [L, NSLOT, 128, SLOT], BF16).ap()
        S.untracked |= {"x", "wpack", "pvec", "bvec", "fnw", "cmat"}
        self.dbg_out = {}
        for name, shape in self.dbg:
            self.dbg_out[name] = dt_(name, shape, F32, kind="ExternalOutput").ap()
        self._alloc()

    def sb(self, name, shape, dtype):
        return self.nc.alloc_sbuf_tensor(name, list(shape), dtype).ap()

    def _alloc(self):
        sb = self.sb
        nc = self.nc
        self.cm = sb("cm", [128, 768], F32)
        self.identb = sb("identb", [128, 128], BF16)
        self.nmb = sb("nmb", [128, 256], BF16)
        self.ugo = sb("ugo", [128, 384], BF16)
        self.ahl = sb("ahl", [128, 2, 64], BF16)
        self.fnw = sb("fnwb", [128, D_MODEL], F32)
        self.pv = {}
        self.bv = {}
        self.Sst = {}
        self.kT = {}
        self.v1 = {}
        self.c_in = {}
        self.xtail = {}
        for l in self.layers:
            self.pv[l] = sb("pv%d" % l, [128, 40], F32)
            self.bv[l] = sb("bv%d" % l, [128, 64], F32)
            self.Sst[l] = sb("S%d" % l, [128, 1024], F32)
            self.kT[l] = sb("kT%d" % l, [128, 640], BF16)
            self.v1[l] = sb("v1_%d" % l, [128, 5, 256], BF16)
            self.c_in[l] = sb("ctail%d" % l, [128, 4, 30], BF16)
            self.xtail[l] = sb("xtail%d" % l, [128, 12, 3], BF16)
        self.ring = [sb("ring%d" % i, [128, SLOT], BF16) for i in range(self.NR)]
        self.ring_cnt = 0
        self.ring_owner = {}
        self.lazyq = []
        self.cin = sb("cin", [128, 4, 542], BF16)
        self.xt = sb("xt", [128, NCH, D_MODEL], F32)
        self.hy = sb("hy", [128, 16, T], BF16)
        self.hT = sb("hT", [128, 8, T], BF16)
        self.junks = [sb("junk%d" % i, [128, 1024], BF16) for i in range(2)]
        self.junk_i = 0
        self.ss = sb("ss", [128, 8], F32)
        self.xbc_in = [sb("xbcin%d" % i, [128, 515], BF16) for i in range(3)]
        self.xbcT = sb("xbcT", [128, 12, T], BF16)
        self.qT = sb("qT", [128, 4, T], BF16)
        self.sig = sb("sig", [128, T], F32)
        self.gzc = sb("gzc", [128, 4, T], BF16)
        self.gz = sb("gz", [128, NCH, 1536], BF16)
        self.dts = sb("dts", [128, 6, 64], F32)
        self.X = sb("X", [128, 2, 2, 1024], BF16)
        self.LT = sb("LT", [128, 2048], BF16)
        self.cbTm = sb("cbTm", [128, 256], BF16)
        self.xs_tm = sb("xs_tm", [128, 1024], BF16)
        self.xdt = sb("xdt", [128, 1024], BF16)
        self.xdd = sb("xdd", [128, 1024], BF16)
        self.B_tm = sb("B_tm", [128, 256], BF16)
        self.Sbf = sb("Sbf", [128, 1024], BF16)
        self.ea = sb("ea", [128, 32], F32)
        self.ybuf = sb("ybuf", [128, 1024], F32)
        self.ytmp = sb("ytmp", [128, 1024], F32)
        self.yssd = sb("yssd", [128, 1024], BF16)
        self.gss = sb("gss", [128, 4], F32)
        self.PT = sb("PT", [128, 4, T], BF16)
        self.oa = sb("oa", [128, 512], F32)
        self.yat = sb("yat", [128, 512], BF16)
        self.den = sb("den", [128, 16], F32)
        self.cv = sb("cv", [128, 4, T], F32)
        self.sqc = [sb("sqc%d" % i, [128, 3, T], BF16) for i in range(1)]
        self.st12 = sb("st12", [128, 2 * T], F32)
        self.st1 = self.st12[:, 0:T]
        self.st2 = self.st12[:, T:2 * T]
        self.xsD = self.st12
        self.ps = nc.alloc_psum_tensor("ps", [128, 8, 512], F32).ap()
        self.psb = self.ps.bitcast(BF16)
        self.ident_f = self.cm[:, 0:128]
        self.U_f = self.cm[:, 128:256]
        self.G_f = self.cm[:, 256:384]
        self.ones_f = self.cm[:, 384:512]
        self.U_b = self.ugo[:, 0:128]
        self.G_b = self.ugo[:, 128:256]
        self.ones_b = self.ugo[:, 256:384]

    @property
    def junk(self):
        self.junk_i += 1
        return self.junks[self.junk_i % 2]

    def mm(self, out, lhsT, rhs, start=True, stop=True):
        self.S.add("pe", lambda e: e.matmul(out, lhsT=lhsT, rhs=rhs, start=start, stop=stop),
                   reads=[lhsT, rhs], writes=[out])

    def tr(self, out, in_):
        ident = self.identb
        self.S.add("pe", lambda e: e.transpose(out, in_, ident), reads=[in_, ident], writes=[out])

    def act(self, out, in_, func, bias=None, scale=None, accum=None):
        kw = {}
        rd = [in_]
        if bias is not None:
            kw["bias"] = bias
            rd.append(bias)
        if scale is not None:
            kw["scale"] = scale
            rd.append(scale)
        wr = [out]
        if accum is not None:
            kw["accum_out"] = accum
            wr.append(accum)
        self.S.add("act", lambda e: e.activation(out=out, in_=in_, func=func, **kw),
                   reads=rd, writes=wr)

    def tt(self, out, in0, in1, op, eng="dve"):
        self.S.add(eng, lambda e: e.tensor_tensor(out=out, in0=in0, in1=in1, op=op),
                   reads=[in0, in1], writes=[out])

    def tsmul(self, out, in0, scalar, eng="dve"):
        self.S.add(eng, lambda e: e.tensor_scalar_mul(out, in0, scalar),
                   reads=[in0, scalar], writes=[out])

    def ts2(self, out, in0, s1, s2, op0, op1, eng="dve"):
        self.S.add(eng, lambda e: e.tensor_scalar(out=out, in0=in0, scalar1=s1, scalar2=s2,
                                                  op0=op0, op1=op1),
                   reads=[in0, s1, s2], writes=[out])

    def stt(self, out, in0, scalar, in1, op0, op1, eng="dve"):
        self.S.add(eng, lambda e: e.scalar_tensor_tensor(out=out, in0=in0, scalar=scalar, in1=in1,
                                                         op0=op0, op1=op1),
                   reads=[in0, scalar, in1], writes=[out])

    def cp(self, out, in_, eng="dve"):
        if eng == "act":
            self.S.add("act", lambda e: e.copy(out, in_), reads=[in_], writes=[out])
        else:
            self.S.add(eng, lambda e: e.tensor_copy(out, in_), reads=[in_], writes=[out])

    def memset(self, ap, val, eng="dve"):
        self.S.add(eng, lambda e: e.memset(ap, val), reads=[], writes=[ap])

    def dma(self, out, in_, q="sp"):
        self.S.add(q, lambda e: e.dma_start(out=out, in_=in_), reads=[in_], writes=[out], dma=True)

    def load_slot(self, l, s):
        idx = self.ring_cnt % self.NR
        own = self.ring_owner.get(idx)
        assert own is None or own["users"] == 0, "ring slot still has pending users"
        r = self.ring[idx]
        self.ring_cnt += 1
        self.ring_owner[idx] = None
        self.dma(r, self.wbf[l, s], q="sp")
        return r

    def lazy_slot(self, l, s):
        h = {"ap": None, "l": l, "s": s, "users": 0}
        self.lazyq.append(h)
        return h

    def try_loads(self):
        while self.lazyq:
            idx = self.ring_cnt % self.NR
            own = self.ring_owner.get(idx)
            if own is not None and own["users"] > 0:
                break
            h = self.lazyq.pop(0)
            h["ap"] = self.load_slot(h["l"], h["s"])
            self.ring_owner[idx] = h

    def dump(self, name, ap):
        if name in self.dbg_out:
            self.dma(self.dbg_out[name], ap, q="pool")

    def cast_layer(self, l):
        for s_ in [11, 12, 0, 1, 2, 10, 7, 8, 3, 9, 5, 4, 6, 13, 14, 15, 16, 17, 18, 19, 20]:
            self.dma(self.wbf[l, s_], self.wpack[l, s_], q="pool")

    def prologue(self):
        cm = self.cm
        xv = self.x_d[0:T, :].rearrange("(j p) d -> p j d", p=128)
        self.dma(self.xt, xv, q="sp")
        self.dma(cm, self.cmat_d, q="sp")
        self.cp(self.identb, cm[:, 0:128])
        self.cp(self.nmb, cm[:, 512:768])
        self.cp(self.ugo, cm[:, 128:512])
        self.dma(self.fnw, self.fnw_d.partition_broadcast(128), q="sp")
        for l in self.layers:
            self.dma(self.pv[l], self.pvec[l], q="sp")
            self.dma(self.bv[l], self.bvec[l:l + 1, :].partition_broadcast(128), q="sp")
        self.cast_layer(self.layers[0])
        for l in self.layers:
            bv = self.bv[l]
            self.act(bv[:, 16:32], bv[:, 16:32], AF.Exp)
            self.tsmul(bv[:, 16:32], bv[:, 16:32], -1.0)
            self.act(bv[:, 48:56], bv[:, 48:56], AF.Exp)
            v1 = self.v1[l]
            self.memset(v1, 1.0)

    def layer_pass(self, sc, l):
        seq_start = (sc % (SEQ // T) == 0)
        ps, psb = self.ps, self.psb
        pv, bv = self.pv[l], self.bv[l]
        xt = self.xt
        hy = self.hy
        h = V(hy, [[1024, 4], [1, 1024]])
        hT = self.hT
        yT = hy
        ss = self.ss

        if seq_start:
            self.memset(self.xtail[l], 0.0)
            self.memset(self.c_in[l], 0.0)
        else:
            self.cp(self.Sbf, self.Sst[l], eng="act")

        abank = (0, 2, 6, 4)
        nw8 = V(pv[:, 0:1], [[1, 8], [0, 128]])
        for j in range(NCH):
            cs = slice(j * 128, (j + 1) * 128)
            self.act(self.junk, xt[:, j, :], AF.Square, accum=ss[:, j:j + 1])
            self.act(ss[:, 4 + j:5 + j], ss[:, j:j + 1], AF.Ln, scale=1.0 / D_MODEL, bias=EPS)
            self.act(ss[:, 4 + j:5 + j], ss[:, 4 + j:5 + j], AF.Exp, scale=-0.5)
            self.tsmul(h[:, j, :], xt[:, j, :], ss[:, 4 + j:5 + j])
            for kc in range(8):
                self.tr(psb[:, abank[j], kc * 128:(kc + 1) * 128], h[:, j, kc * 128:(kc + 1) * 128])
            self.tt(hT[:, :, cs], V(psb[:, abank[j], 0:1], [[128, 8], [1, 128]]), nw8, ALU.mult)
        if STOP == "A":
            return
        bank_rot = [0]

        def nbank(lo, n):
            b = lo + bank_rot[0] % n
            bank_rot[0] += 1
            return b

        def fm_block(slot, c0, bank):
            s3 = V(slot, [[512, 8], [1, 512]])
            for kc in range(8):
                self.mm(ps[:, bank, :], s3[:, kc, c0:c0 + 128], hT[:, kc, :],
                        start=(kc == 0), stop=(kc == 7))

        d4 = [self.load_slot(l, 11), self.load_slot(l, 12)]
        xtail = self.xtail[l]
        xslots = {}

        def xbc_front(b):
            g, sub = divmod(b, 4)
            if g not in xslots:
                xslots[g] = self.load_slot(l, g)
            bank = nbank(0, 4)
            fm_block(xslots[g], sub * 128, bank)
            xin = self.xbc_in[b % 3]
            self.cp(xin[:, 0:3], xtail[:, b, :], eng="dve")
            self.cp(xin[:, 3:515], ps[:, bank, :], eng="act")
            self.cp(xtail[:, b, :], xin[:, 512:515], eng="dve")

        def xbc_conv(b):
            xin = self.xbc_in[b % 3]
            cbank = 4 + (b % 2)
            dsl = d4[b // 6]
            bb = b % 6
            for k in range(4):
                self.mm(ps[:, cbank, :], dsl[:, (bb * 4 + k) * 128:(bb * 4 + k + 1) * 128],
                        xin[:, k:k + T], start=(k == 0), stop=(k == 3))
            self.act(self.xbcT[:, b, :], ps[:, cbank, :], AF.Silu, bias=pv[:, 16 + b:17 + b])

        xbc_front(0)
        for b in range(12):
            if b + 1 < 12:
                xbc_front(b + 1)
            xbc_conv(b)
        slot = self.load_slot(l, 10)
        s3 = V(slot, [[512, 8], [1, 512]])
        kT, v1 = self.kT[l], self.v1[l]
        bank = nbank(0, 4)
        fm_block(slot, 0, bank)
        self.cp(kT[:, 128:640], ps[:, bank, :], eng="act")
        dts = self.dts
        for j in range(NCH):
            vb = 6 + (j % 2)
            for kc in range(8):
                self.mm(ps[:, vb, 0:144], hT[:, kc, j * 128:(j + 1) * 128], s3[:, kc, 128:272],
                        start=(kc == 0), stop=(kc == 7))
            self.cp(V(v1[:, 1 + j, 0:1], [[128, 2], [1, 64]]), V(ps[:, vb, 0:1], [[64, 2], [1, 64]]),
                    eng="dve")
            self.cp(dts[:, 0, j * 16:(j + 1) * 16], ps[:, vb, 128:144], eng="dve")
        for g3 in range(2):
            slot = self.load_slot(l, 7 + g3)
            s3 = V(slot, [[512, 8], [1, 512]])
            for j in range(NCH):
                bank = nbank(0, 4)
                for kc in range(8):
                    self.mm(ps[:, bank, :], hT[:, kc, j * 128:(j + 1) * 128], s3[:, kc, :],
                            start=(kc == 0), stop=(kc == 7))
                self.act(self.gz[:, j, g3 * 512:(g3 + 1) * 512], ps[:, bank, :], AF.Silu)
        dtraw, dtx, dab, de, dtt, da = [dts[:, i, :] for i in range(6)]
        b16 = lambda c0: V(bv[:, c0:c0 + 1], [[0, 4], [1, 16]])
        v416 = lambda a: V(a, [[16, 4], [1, 16]])
        self.tt(v416(dtx), v416(dtraw), b16(0), ALU.add)
        self.tsmul(dab, dtx, -1.0)
        self.tt(dab, dab, dtx, ALU.min)
        self.act(de, dab, AF.Exp)
        self.act(de, de, AF.Ln, bias=1.0)
        self.stt(dtt, dtx, 0.0, de, ALU.max, ALU.add)
        self.tt(v416(da), v416(dtt), b16(16), ALU.mult)
        self.cp(self.ahl[:, 0, :], da, eng="dve")
        self.tt(self.ahl[:, 1, :], da, self.ahl[:, 0, :], ALU.subtract)
        self.build_X(0)

        if sc == 0 and l == self.layers[0]:
            for l2 in self.layers[1:]:
                self.cast_layer(l2)
        c_in = self.cin
        cv = self.cv
        self.cp(c_in[:, :, 0:30], self.c_in[l], eng="dve")
        fq = self.fq = []
        self.f_inflight = []
        self.f_free = [0, 1]
        sl_q = self.lazy_slot(l, 3)
        sl_za = self.lazy_slot(l, 9)
        sl_g = self.lazy_slot(l, 5)
        sl_a = self.lazy_slot(l, 4)
        sl_zc = self.lazy_slot(l, 6)

        def u_fm(tag, slot, c0, evac):
            slot["users"] += 1

            def mmf(bank):
                fm_block(slot["ap"], c0, bank)
                slot["users"] -= 1
            fq.append((tag, 8, mmf, evac))

        def u_za(j):
            sl_za["users"] += 1

            def mmf(bank):
                s3z = V(sl_za["ap"], [[512, 8], [1, 512]])
                for kc in range(8):
                    self.mm(ps[:, bank, :], hT[:, kc, j * 128:(j + 1) * 128], s3z[:, kc, :],
                            start=(kc == 0), stop=(kc == 7))
                sl_za["users"] -= 1
            fq.append(("za%d" % j, 8, mmf,
                       lambda bank: self.act(self.gz[:, j, 1024:1536], ps[:, bank, :], AF.Silu)))

        def u_glu(b):
            u_fm("cg%d" % b, sl_g, b * 128,
                 lambda bank: self.act(self.sig, ps[:, bank, :], AF.Tanh, scale=0.5))
            u_fm("ca%d" % b, sl_a, b * 128,
                 lambda bank: self.stt(c_in[:, b, 30:542], self.sig, 1.0, ps[:, bank, :],
                                       ALU.add, ALU.mult))

        for b in range(4):
            u_fm("q%d" % b, sl_q, b * 128,
                 (lambda bank, b=b: self.cp(self.qT[:, b, :], ps[:, bank, :], eng="act")))
        u_za(0)
        u_glu(0)
        u_glu(1)
        u_za(1)
        u_glu(2)
        u_glu(3)
        u_za(2)
        for b in range(4):
            u_fm("zc%d" % b, sl_zc, b * 128,
                 (lambda bank, b=b: self.act(self.gzc[:, b, :], ps[:, bank, :], AF.Silu)))
        u_za(3)
        d31 = {}

        def u_conv(b):
            d31[b] = self.lazy_slot(l, 13 + b)
            d31[b]["users"] += 1

            def mmf(bank):
                for k in range(31):
                    self.mm(ps[:, bank, :], d31[b]["ap"][:, k * 128:(k + 1) * 128], c_in[:, b, k:k + T],
                            start=(k == 0), stop=(k == 30))
                d31[b]["users"] -= 1
            fq.append(("cv%d" % b, 31, mmf,
                       lambda bank: self.act(cv[:, b, :], ps[:, bank, :], AF.Identity, scale=0.5,
                                             bias=pv[:, 28 + b:29 + b])))
        for b in range(4):
            u_conv(b)
        self.try_loads()

        if STOP == "B":
            self.fill(10 ** 6)
            self.fill(10 ** 6)
            return
        for j in range(NCH):
            self.ssd_chunk(l, j, first=(seq_start and j == 0))
            self.ensure(["q0", "q1", "q2", "q3", "za%d" % j])
            self.attn_chunk(l, j, first=(seq_start and j == 0))
        while self.fq or self.f_inflight:
            self.fill(10 ** 6)
        self.cp(kT[:, 0:128], kT[:, 512:640], eng="dve")
        self.cp(v1[:, 0, :], v1[:, 4, :], eng="dve")

        wo = [self.load_slot(l, 17 + q) for q in range(4)]
        obank = {0: (0, 1), 1: (2, 3), 2: (6, 7), 3: (4, 5)}

        def oproj(js, qs):
            for q in qs:
                w3 = V(wo[q], [[1024, 4], [1, 1024]])
                for j in js:
                    for half in range(2):
                        for mi in range(4):
                            mb = q * 4 + mi
                            self.mm(ps[:, obank[j][half], :], yT[:, mb, j * 128:(j + 1) * 128],
                                    w3[:, mi, half * 512:(half + 1) * 512],
                                    start=(mb == 0), stop=(mb == 15))

        oproj([0, 1], [0, 1, 2])
        for b in range(4):
            sq = self.sqc[0]
            self.act(sq[:, 0, :], cv[:, b, :], AF.Square)
            self.cp(sq[:, 1, :], cv[:, b, :], eng="dve")
            self.tt(sq[:, 2, :], cv[:, b, :], sq[:, 1, :], ALU.subtract)
            self.mm(ps[:, 4, :], self.ones_b, sq[:, 1, :], start=(b == 0), stop=False)
            self.mm(ps[:, 4, :], self.ones_b, sq[:, 2, :], start=False, stop=(b == 3))
            self.mm(ps[:, 5, :], self.ones_b, sq[:, 0, :], start=(b == 0), stop=(b == 3))
        self.cp(self.c_in[l], c_in[:, :, 512:542], eng="dve")
        st1, st2 = self.st1, self.st2
        self.tsmul(st1, ps[:, 4, :], 1.0 / 512)
        self.tt(st2, st1, st1, ALU.mult)
        self.stt(st2, ps[:, 5, :], 1.0 / 512, st2, ALU.mult, ALU.subtract)
        oproj([2, 3], [0, 1, 2])
        self.act(st2, st2, AF.Ln, bias=EPS)
        self.act(st2, st2, AF.Exp, scale=-0.5)
        for b in range(4):
            self.tt(cv[:, b, :], cv[:, b, :], st1, ALU.subtract)
            self.tt(cv[:, b, :], cv[:, b, :], st2, ALU.mult)
            self.act(cv[:, b, :], cv[:, b, :], AF.Silu, scale=pv[:, 32 + b:33 + b],
                     bias=pv[:, 36 + b:37 + b])
            self.tt(yT[:, 12 + b, :], cv[:, b, :], self.gzc[:, b, :], ALU.mult)
        oproj([0, 1, 2, 3], [3])
        for j in range(NCH):
            for half in range(2):
                xs = xt[:, j, half * 512:(half + 1) * 512]
                self.tt(xs, ps[:, obank[j][half], :], xs, ALU.add)

    def build_X(self, j):
        Ub8 = V(self.U_b, [[0, 8], [1, 128]])
        for hh in range(2):
            for hl in range(2):
                ab = V(self.ahl[:, hl, j * 16 + hh * 8:j * 16 + hh * 8 + 1], [[1, 8], [0, 128]])
                self.tt(V(self.X[:, hh, hl, 0:1], [[128, 8], [1, 128]]), ab, Ub8, ALU.mult, eng="dve")

    def fill(self, budget):
        for evac, bank in self.f_inflight:
            evac(bank)
            self.f_free.append(bank)
        self.f_inflight = []
        self.try_loads()
        while budget > 0 and self.fq and self.f_free:
            tag, cost, mmf, evac = self.fq.pop(0)
            bank = self.f_free.pop(0)
            mmf(bank)
            self.f_inflight.append((evac, bank))
            budget -= cost
            self.try_loads()

    def ensure(self, tags):
        while any(t[0] in tags for t in self.fq):
            self.fill(8)
        self.fill(0)

    def ssd_chunk(self, l, j, first):
        ps, psb = self.ps, self.psb
        pv, bv = self.pv[l], self.bv[l]
        xbcT = self.xbcT
        cs = slice(j * 128, (j + 1) * 128)
        dts = self.dts
        dt_j = dts[:, 4, j * 16:(j + 1) * 16]
        a_j = dts[:, 5, j * 16:(j + 1) * 16]
        Sst = self.Sst[l]
        yT = self.hy
        for b in range(8):
            self.tr(psb[:, 7, b * 128:(b + 1) * 128], xbcT[:, b, cs])
        for g in range(2):
            self.tr(psb[:, 6, 512 + g * 128:512 + (g + 1) * 128], xbcT[:, 8 + g, cs])
        self.cp(self.xs_tm, psb[:, 7, :], eng="act")
        dtb = V(dt_j, [[1, 16], [0, 64]])
        self.tt(V(self.xdt, [[64, 16], [1, 64]]), V(self.xs_tm, [[64, 16], [1, 64]]), dtb, ALU.mult)
        self.cp(self.B_tm, psb[:, 6, 512:768], eng="act")
        Db = V(bv[:, 32:33], [[1, 16], [0, 64]])
        if STOP == "E1":
            return
        for g in range(2):
            self.mm(ps[:, 6, g * 128:(g + 1) * 128], xbcT[:, 8 + g, cs], xbcT[:, 10 + g, cs])
        Ub = V(self.U_b, [[0, 2], [1, 128]])
        for hl in range(2):
            a_hl = self.ahl[:, hl, j * 16:(j + 1) * 16]
            self.mm(ps[:, 6, 384:400], self.U_b, a_hl, start=(hl == 0), stop=(hl == 1))
        for hl in range(2):
            a_hl = self.ahl[:, hl, j * 16:(j + 1) * 16]
            self.mm(ps[:, 6, 400:416], self.ones_b, a_hl, start=(hl == 0), stop=(hl == 1))
        self.tt(V(self.cbTm, [[128, 2], [1, 128]]), V(ps[:, 6, 0:1], [[128, 2], [1, 128]]), Ub, ALU.mult)
        self.act(self.ea, ps[:, 6, 384:416], AF.Exp)
        if STOP == "E2":
            return
        for hh in range(2):
            for q in range(2):
                for hl in range(2):
                    self.mm(ps[:, 2 + 2 * hh + q, :], self.G_b, self.X[:, hh, hl, q * 512:(q + 1) * 512],
                            start=(hl == 0), stop=(hl == 1))
            self.act(self.LT[:, hh * 1024:(hh + 1) * 1024],
                     V(ps[:, 2 + 2 * hh, 0:1], [[1, 1024]]), AF.Exp)
        self.fill(20)
        dte = V(self.LT[:, 127:128], [[128, 16], [0, 64]])
        self.tt(V(self.xdd, [[64, 16], [1, 64]]), V(self.xdt, [[64, 16], [1, 64]]), dte, ALU.mult)
        LT4 = V(self.LT, [[1024, 2], [128, 8], [1, 128]])
        cb4 = V(self.cbTm, [[128, 2], [0, 8], [1, 128]])
        self.tt(LT4, LT4, cb4, ALU.mult)
        Db = V(bv[:, 32:33], [[1, 16], [0, 64]])
        xsD_o, xs_i, mt_dep = V(self.xsD, [[64, 16], [1, 64]]), V(self.xs_tm, [[64, 16], [1, 64]]), self.LT
        self.S.add("pool", lambda e: e.tensor_tensor(out=xsD_o, in0=xs_i, in1=Db, op=ALU.mult),
                   reads=[xs_i, Db, mt_dep], writes=[xsD_o])
        if STOP == "E4":
            return
        for hd in range(16):
            self.mm(ps[:, 6 + hd // 8, (hd % 8) * 64:(hd % 8 + 1) * 64],
                    self.LT[:, hd * 128:(hd + 1) * 128], self.xdt[:, hd * 64:(hd + 1) * 64])
        if not first:
            for g in range(2):
                self.mm(ps[:, 2 + g, :], xbcT[:, 10 + g, cs], self.Sbf[:, g * 512:(g + 1) * 512])
        for g in range(2):
            self.mm(ps[:, 4 + g, :], self.B_tm[:, g * 128:(g + 1) * 128],
                    self.xdd[:, g * 512:(g + 1) * 512])
        self.fill(40)
        ybuf, ytmp = self.ybuf, self.ytmp
        yd = V(ps[:, 6, 0:1], [[1, 1024]])
        if not first:
            eab = V(self.ea[:, 0:1], [[1, 16], [0, 64]])
            self.tt(V(ytmp, [[64, 16], [1, 64]]), V(ps[:, 2, 0:1], [[64, 16], [1, 64]]), eab, ALU.mult)
            self.tt(ybuf, yd, ytmp, ALU.add)
        else:
            self.cp(ybuf, yd, eng="dve")
        self.tt(ybuf, ybuf, self.xsD, ALU.add)
        snew = V(ps[:, 4, 0:1], [[1, 1024]])
        if not first:
            cdb = V(self.ea[:, 16:17], [[1, 16], [0, 64]])
            S3 = V(Sst, [[64, 16], [1, 64]])
            self.tt(S3, S3, cdb, ALU.mult, eng="pool")
            self.tt(Sst, Sst, snew, ALU.add)
        else:
            self.cp(Sst, snew, eng="dve")
        self.cp(self.Sbf, Sst, eng="act")
        if STOP == "E6":
            return
        self.tt(ybuf, ybuf, self.gz[:, j, 0:1024], ALU.mult)
        if j + 1 < NCH:
            self.build_X(j + 1)
        gss = self.gss
        for g in range(2):
            self.act(self.junk[:, 0:512], ybuf[:, g * 512:(g + 1) * 512], AF.Square,
                     accum=gss[:, g:g + 1])
        self.act(gss[:, 2:4], gss[:, 0:2], AF.Ln, scale=1.0 / 512, bias=EPS)
        self.act(gss[:, 2:4], gss[:, 2:4], AF.Exp, scale=-0.5)
        for g in range(2):
            self.act(self.yssd[:, g * 512:(g + 1) * 512], ybuf[:, g * 512:(g + 1) * 512],
                     AF.Identity, scale=gss[:, 2 + g:3 + g])
        for b in range(8):
            self.tr(psb[:, 2, b * 128:(b + 1) * 128], self.yssd[:, b * 128:(b + 1) * 128])
        snw = V(pv[:, 8:9], [[1, 8], [0, 128]])
        self.tt(yT[:, 0:8, cs], V(psb[:, 2, 0:1], [[128, 8], [1, 128]]), snw, ALU.mult)

    def attn_chunk(self, l, j, first):
        ps, psb = self.ps, self.psb
        bv = self.bv[l]
        kT, v1 = self.kT[l], self.v1[l]
        qT = self.qT
        yT = self.hy
        cs = slice(j * 128, (j + 1) * 128)
        kbs = [1] if first else [0, 1]
        for g in range(2):
            prt = slice(g * 64, (g + 1) * 64)
            for kb in kbs:
                bank = 2 + g * 2 + kb
                kcols = slice((j + kb) * 128, (j + kb + 1) * 128)
                self.mm(ps[:, bank, :], kT[prt, kcols], V(qT[prt, 0, cs], [[T, 4], [1, 128]]),
                        start=True, stop=False)
                nm = V(self.nmb[:, (1 - kb) * 128:(1 - kb) * 128 + 1], [[0, 4], [1, 128]])
                self.mm(ps[:, bank, :], self.identb, nm, start=False, stop=True)
                self.act(self.PT[:, bank - 2, :], ps[:, bank, :], AF.Exp, scale=0.125)
        self.fill(12)
        for g in range(2):
            for r in range(4):
                hd = 4 * g + r
                o = ps[:, 6 + hd // 4, (hd % 4) * 128:(hd % 4) * 128 + 66]
                for n, kb in enumerate(kbs):
                    self.mm(o, self.PT[:, g * 2 + kb, r * 128:(r + 1) * 128],
                            v1[:, j + kb, g * 128:g * 128 + 66],
                            start=(n == 0), stop=(n == len(kbs) - 1))
        self.fill(8)
        o4 = V(ps[:, 6, 0:1], [[512, 2], [128, 4], [1, 64]])
        osum = V(ps[:, 6, 64:65], [[512, 2], [128, 4]])
        den = self.den
        self.tt(V(den[:, 0:1], [[4, 2], [1, 4]]), osum, V(bv[:, 48:49], [[4, 2], [1, 4]]), ALU.add)
        self.S.add("dve", lambda e: e.reciprocal(den[:, 8:16], den[:, 0:8]),
                   reads=[den[:, 0:8]], writes=[den[:, 8:16]])
        rb = V(den[:, 8:9], [[4, 2], [1, 4], [0, 64]])
        self.tt(V(self.oa, [[256, 2], [64, 4], [1, 64]]), o4, rb, ALU.mult)
        self.tt(self.yat, self.oa, self.gz[:, j, 1024:1536], ALU.mult)
        for b in range(4):
            self.tr(psb[:, 2, b * 128:(b + 1) * 128], self.yat[:, b * 128:(b + 1) * 128])
        self.cp(yT[:, 8:12, cs], V(psb[:, 2, 0:1], [[128, 4], [1, 128]]), eng="act")

    def finish_sc(self, sc):
        xt, ss = self.xt, self.ss
        xos = [self.ybuf, self.ytmp]
        for j in range(NCH):
            rows = self.y_d[sc * T + j * 128: sc * T + (j + 1) * 128, :]
            if self.final_norm:
                self.act(self.junk, xt[:, j, :], AF.Square, accum=ss[:, j:j + 1])
                self.act(ss[:, 4 + j:5 + j], ss[:, j:j + 1], AF.Ln, scale=1.0 / D_MODEL, bias=EPS)
                self.act(ss[:, 4 + j:5 + j], ss[:, 4 + j:5 + j], AF.Exp, scale=-0.5)
                xo = xos[j % 2]
                self.stt(xo, xt[:, j, :], ss[:, 4 + j:5 + j], self.fnw, ALU.mult, ALU.mult)
                self.dma(rows, xo, q="pool")
            else:
                self.dma(rows, xt[:, j, :], q="pool")
            if sc + 1 < self.n_sc:
                nrows = self.x_d[(sc + 1) * T + j * 128:(sc + 1) * T + (j + 1) * 128, :]
                self.dma(xt[:, j, :], nrows, q="pool")

    def build(self):
        self.prologue()
        for sc in range(self.n_sc):
            xv = self.x_d[sc * T:(sc + 1) * T, :].rearrange("(j p) d -> p j d", p=128)
            for l in self.layers:
                self.layer_pass(sc, l)
            self.finish_sc(sc)
        self.stack = ExitStack()
        self.S.emit(self.stack)
        print("sbuf bytes remaining", self.nc.sbuf_bytes_remaining, "ops", len(self.S.ops))
        return self.nc


def _consts():
    cm = np.zeros((128, 768), np.float32)
    idx = np.arange(128)
    cm[:, 0:128] = np.eye(128, dtype=np.float32)
    cm[:, 128:256] = (idx[:, None] <= idx[None, :])
    cm[:, 256:384] = (idx[:, None] > idx[None, :])
    cm[:, 384:512] = 1.0
    cm[:, 512:640] = np.where(idx[:, None] > idx[None, :], NEG, 0.0)
    cm[:, 640:768] = np.where(idx[:, None] <= idx[None, :], NEG, 0.0)
    return cm


def pack_weights(w_in, w_out, ssd_conv_w, conf_dw_w):
    L = w_in.shape[0]
    wp = np.zeros((L, NSLOT, 128, SLOT), np.float32)
    zc, xc, dtc, qc, kc_, vc, cc = 0, 2048, 3584, 3600, 4112, 4240, 4368
    qperm = []
    for b in range(4):
        qperm += list(range(qc + b * 64, qc + (b + 1) * 64))
        qperm += list(range(qc + (4 + b) * 64, qc + (5 + b) * 64))
    groups = [
        list(range(xc, xc + 512)), list(range(xc + 512, xc + 1024)), list(range(xc + 1024, xc + 1536)),
        qperm,
        list(range(cc, cc + 512)), list(range(cc + 512, cc + 1024)),
        list(range(zc + 1536, zc + 2048)),
        list(range(zc, zc + 512)), list(range(zc + 512, zc + 1024)), list(range(zc + 1024, zc + 1536)),
        list(range(kc_, kc_ + 128)) + list(range(vc, vc + 128)) + list(range(dtc, dtc + 16)),
    ]
    for l in range(L):
        wl = w_in[l].reshape(8, 128, -1)
        for g, cols in enumerate(groups):
            blk = np.zeros((128, 8, 512), np.float32)
            blk[:, :, :len(cols)] = wl[:, :, cols].transpose(1, 0, 2)
            wp[l, g] = blk.reshape(128, SLOT)
        cw = ssd_conv_w[l]
        for b in range(12):
            s, bb = divmod(b, 6)
            d = wp[l, 11 + s].reshape(128, 32, 128)
            for k in range(4):
                d[np.arange(128), bb * 4 + k, np.arange(128)] = cw[k, b * 128:(b + 1) * 128]
        dw = conf_dw_w[l]
        for b in range(4):
            d = wp[l, 13 + b].reshape(128, 32, 128)
            for k in range(31):
                d[np.arange(128), k, np.arange(128)] = dw[k, b * 128:(b + 1) * 128]
        wo = w_out[l].reshape(16, 128, 1024)
        for q in range(4):
            wp[l, 17 + q] = wo[4 * q:4 * q + 4].transpose(1, 0, 2).reshape(128, SLOT)
    return wp


def pack_vecs(norm_w, ssd_norm_w, ssd_conv_b, conf_dw_b, conf_ln_w, conf_ln_b,
              ssd_dt_bias, ssd_a_log, ssd_d, attn_sinks):
    L = norm_w.shape[0]
    pvec = np.zeros((L, 128, 40), np.float32)
    bvec = np.zeros((L, 64), np.float32)
    for l in range(L):
        pvec[l, :, 0:8] = norm_w[l].reshape(8, 128).T
        pvec[l, :, 8:16] = ssd_norm_w[l].reshape(8, 128).T
        pvec[l, :, 16:28] = ssd_conv_b[l].reshape(12, 128).T
        pvec[l, :, 28:32] = conf_dw_b[l].reshape(4, 128).T
        pvec[l, :, 32:36] = conf_ln_w[l].reshape(4, 128).T
        pvec[l, :, 36:40] = conf_ln_b[l].reshape(4, 128).T
        bvec[l, 0:16] = ssd_dt_bias[l]
        bvec[l, 16:32] = ssd_a_log[l]
        bvec[l, 32:48] = ssd_d[l]
        bvec[l, 48:56] = attn_sinks[l]
    return pvec, bvec


_PROG_CACHE = {}


def get_program(n_sc, layers, final_norm):
    key = (n_sc, tuple(layers), final_norm)
    if key not in _PROG_CACHE:
        b = Builder(n_sc, layers, final_norm)
        b.build()
        _PROG_CACHE[key] = b
    return _PROG_CACHE[key]


def kernel(x, norm_w, w_in, ssd_conv_w, ssd_conv_b, ssd_dt_bias, ssd_a_log, ssd_d,
           ssd_norm_w, attn_sinks, conf_dw_w, conf_dw_b, conf_ln_w, conf_ln_b, w_out,
           final_norm_w):
    x = np.ascontiguousarray(np.asarray(x, dtype=np.float32))
    args = [np.asarray(a, dtype=np.float32) for a in
            (norm_w, w_in, ssd_conv_w, ssd_conv_b, ssd_dt_bias, ssd_a_log, ssd_d, ssd_norm_w,
             attn_sinks, conf_dw_w, conf_dw_b, conf_ln_w, conf_ln_b, w_out, final_norm_w)]
    (norm_w, w_in, ssd_conv_w, ssd_conv_b, ssd_dt_bias, ssd_a_log, ssd_d, ssd_norm_w,
     attn_sinks, conf_dw_w, conf_dw_b, conf_ln_w, conf_ln_b, w_out, final_norm_w) = args
    wp = pack_weights(w_in, w_out, ssd_conv_w, conf_dw_w)
    pvec, bvec = pack_vecs(norm_w, ssd_norm_w, ssd_conv_b, conf_dw_b, conf_ln_w, conf_ln_b,
                           ssd_dt_bias, ssd_a_log, ssd_d, attn_sinks)
    cm = _consts()
    fnw = final_norm_w.reshape(1, D_MODEL)
    seq_per_core = BATCH // NCORES
    n_sc = seq_per_core * SEQ // T
    prog = get_program(n_sc, (0, 1), True)
    in_maps = []
    for c in range(NCORES):
        xs = x[c * seq_per_core:(c + 1) * seq_per_core].reshape(-1, D_MODEL)
        in_maps.append({"x": xs, "wpack": wp, "pvec": pvec, "bvec": bvec, "fnw": fnw, "cmat": cm})
    res = run_bass_kernel_spmd(prog.nc, in_maps, core_ids=list(range(NCORES)))
    out = np.concatenate([r["y"].reshape(seq_per_core, SEQ, D_MODEL) for r in res.results], axis=0)
    return out.astype(np.float32)
```
